# Optimizing a Trainium2 kernel written in Bass

```python
import math
import jax, jax.numpy as jnp
from jax import lax
import numpy as np

D_MODEL = 1024
BATCH = 16
SEQ = 4096
DEPTH = 1

GRID_W = 64
HY_WIDTH = 512
HY_ORDER = 2
FILT_EMB = 33
FILT_BANDS = (FILT_EMB - 1) // 2
FILT_HID = 64
N_Q_HEADS = 8
N_KV_HEADS = 2
GROUP = N_Q_HEADS // N_KV_HEADS
HEAD_DIM = 64
AXIS_DIM = HEAD_DIM // 2
Q_BLOCK = 128
ROPE_THETA = 10000.0
FF_DIM = 2816
PLE_DIM = 256
EPS = 1e-6

ATT_Q = N_Q_HEADS * HEAD_DIM
ATT_KV = N_KV_HEADS * HEAD_DIM
IN_COLS = 3 * HY_WIDTH + ATT_Q + 2 * ATT_KV + 2 * D_MODEL
SPLITS = [3 * HY_WIDTH,
          3 * HY_WIDTH + ATT_Q,
          3 * HY_WIDTH + ATT_Q + ATT_KV,
          3 * HY_WIDTH + ATT_Q + 2 * ATT_KV,
          3 * HY_WIDTH + ATT_Q + 2 * ATT_KV + D_MODEL]

kernel_name = "hyena_gqa_gated_hybrid_encoder"


def rms_norm(x, g):
    xf = x.astype(jnp.float32)
    y = xf * lax.rsqrt(jnp.mean(xf * xf, axis=-1, keepdims=True) + EPS)
    return (y * g.astype(jnp.float32)).astype(x.dtype)


def swiglu(x, w_gate, w_up, w_down):
    return (jax.nn.silu(x @ w_gate) * (x @ w_up)) @ w_down


def short_conv(z, w, b):
    zp = jnp.pad(z, ((0, 0), (1, 1), (0, 0)))
    return zp[:, :-2] * w[0] + zp[:, 1:-1] * w[1] + zp[:, 2:] * w[2] + b


def hyena_filter_freq(L, w1, b1, f1, w2, b2, f2, w3, deltas):
    f32 = jnp.float32
    t = jnp.linspace(0.0, 1.0, L, dtype=f32)[:, None]
    w = (2.0 * math.pi / L) * jnp.arange(L, dtype=f32)
    bands = jnp.linspace(1e-4, FILT_BANDS - 1, FILT_BANDS, dtype=f32)
    ang = w[:, None] * bands[None, :]
    z = jnp.concatenate([t, jnp.cos(ang), -jnp.sin(ang)], axis=-1)
    h = jnp.sin(f1.astype(f32) * (z @ w1.astype(f32) + b1.astype(f32)))
    h = jnp.sin(f2.astype(f32) * (h @ w2.astype(f32) + b2.astype(f32)))
    h = (h @ w3.astype(f32)).reshape(L, HY_ORDER, 2, HY_WIDTH)
    h = h * jnp.exp(-t[:, :, None, None] * jnp.abs(deltas.astype(f32)))
    h_fwd, h_bwd = h[:, :, 0], h[:, :, 1]
    filt = jnp.concatenate([h_fwd[:1] + h_bwd[:1], h_fwd[1:],
                            jnp.zeros((1, HY_ORDER, HY_WIDTH), f32),
                            h_bwd[:0:-1]], axis=0)
    filt = filt / jnp.sum(jnp.abs(filt), axis=0, keepdims=True)
    return jnp.fft.rfft(filt, axis=0)


def long_conv(z, filt_f, bias):
    L = z.shape[1]
    zf32 = z.astype(jnp.float32)
    zf = jnp.fft.rfft(zf32, n=2 * L, axis=1)
    y = jnp.fft.irfft(zf * filt_f[None], n=2 * L, axis=1)[:, :L]
    return (y + zf32 * bias.astype(jnp.float32)).astype(z.dtype)


def hyena_branch(hy_in, short_w, short_b, filt_f, bias):
    z = short_conv(hy_in, short_w, short_b)
    hv, g1, g2 = jnp.split(z, 3, axis=-1)
    z = g1 * long_conv(hv, filt_f[:, 0], bias[0])
    return g2 * long_conv(z, filt_f[:, 1], bias[1])


def axial_rope(L):
    rows = L // GRID_W
    row = jnp.repeat(jnp.arange(rows, dtype=jnp.float32), GRID_W)
    col = jnp.tile(jnp.arange(GRID_W, dtype=jnp.float32), rows)
    inv = ROPE_THETA ** (-jnp.arange(0, AXIS_DIM, 2, dtype=jnp.float32) / AXIS_DIM)
    ang = jnp.concatenate([row[:, None] * inv, col[:, None] * inv], axis=-1)
    return jnp.cos(ang), jnp.sin(ang)


def apply_rope(x, cos, sin):
    xf = x.astype(jnp.float32).reshape(*x.shape[:-1], HEAD_DIM // 2, 2)
    x0, x1 = xf[..., 0], xf[..., 1]
    c = cos[None, :, None, :]
    s = sin[None, :, None, :]
    out = jnp.stack([x0 * c - x1 * s, x0 * s + x1 * c], axis=-1)
    return out.reshape(x.shape).astype(x.dtype)


def attention_branch(q, k, v, q_gain, k_gain, cos, sin):
    B, L, _ = q.shape
    q = q.reshape(B, L, N_Q_HEADS, HEAD_DIM)
    k = k.reshape(B, L, N_KV_HEADS, HEAD_DIM)
    v = v.reshape(B, L, N_KV_HEADS, HEAD_DIM)
    q = apply_rope(rms_norm(q, q_gain), cos, sin)
    k = apply_rope(rms_norm(k, k_gain), cos, sin)
    n_blocks = L // Q_BLOCK
    qb = q.reshape(B, n_blocks, Q_BLOCK, N_KV_HEADS, GROUP, HEAD_DIM)
    qb = jnp.moveaxis(qb, 1, 0)
    scale = HEAD_DIM ** -0.5

    def attend(q_blk):
        s = jnp.einsum("bqkgd,bskd->bkgqs", q_blk, k,
                       preferred_element_type=jnp.float32) * scale
        w = jax.nn.softmax(s, axis=-1).astype(v.dtype)
        return jnp.einsum("bkgqs,bskd->bqkgd", w, v)

    o = lax.map(attend, qb)
    return jnp.moveaxis(o, 0, 1).reshape(B, L, ATT_Q)


def setup_inputs(seed: int = 0) -> dict:
    key = jax.random.key(seed)
    keys = iter(list(jax.random.split(key, 40)))
    f32 = jnp.float32

    def normal(shape, scale):
        return jax.random.normal(next(keys), shape, f32) * scale

    def gain(n):
        return 1.0 + 0.05 * normal((DEPTH, n), 1.0)

    min_decay = math.log(1e-2) / 1.5
    max_decay = math.log(1e-2) / 0.3
    base_deltas = jnp.linspace(min_decay, max_decay, HY_WIDTH, dtype=f32)

    inputs = {}
    inputs["x"] = normal((BATCH, SEQ, D_MODEL), 1.0)
    inputs["p"] = normal((DEPTH, BATCH, SEQ, PLE_DIM), 1.0)
    inputs["ffn1_norm_pre"] = gain(D_MODEL)
    inputs["ffn1_norm_post"] = gain(D_MODEL)
    inputs["ffn1_w_gate"] = normal((DEPTH, D_MODEL, FF_DIM), D_MODEL ** -0.5)
    inputs["ffn1_w_up"] = normal((DEPTH, D_MODEL, FF_DIM), D_MODEL ** -0.5)
    inputs["ffn1_w_down"] = normal((DEPTH, FF_DIM, D_MODEL), FF_DIM ** -0.5)
    inputs["mix_norm_pre"] = gain(D_MODEL)
    inputs["mix_norm_post"] = gain(D_MODEL)
    inputs["w_in"] = normal((DEPTH, D_MODEL, IN_COLS), D_MODEL ** -0.5)
    inputs["hy_short_w"] = normal((DEPTH, 3, 3 * HY_WIDTH), 3 ** -0.5)
    inputs["hy_short_b"] = normal((DEPTH, 3 * HY_WIDTH), 0.02)
    inputs["filt_w1"] = normal((DEPTH, FILT_EMB, FILT_HID), FILT_EMB ** -0.5)
    inputs["filt_b1"] = normal((DEPTH, FILT_HID), 0.1)
    inputs["filt_freq1"] = 1.0 + 0.01 * normal((DEPTH, FILT_HID), 1.0)
    inputs["filt_w2"] = normal((DEPTH, FILT_HID, FILT_HID), FILT_HID ** -0.5)
    inputs["filt_b2"] = normal((DEPTH, FILT_HID), 0.1)
    inputs["filt_freq2"] = 1.0 + 0.01 * normal((DEPTH, FILT_HID), 1.0)
    inputs["filt_w3"] = normal((DEPTH, FILT_HID, HY_ORDER * 2 * HY_WIDTH), FILT_HID ** -0.5)
    inputs["filt_deltas"] = base_deltas + 0.01 * normal((DEPTH, HY_ORDER, 2, HY_WIDTH), 1.0)
    inputs["hy_bias"] = normal((DEPTH, HY_ORDER, HY_WIDTH), 1.0)
    inputs["q_norm"] = gain(HEAD_DIM)
    inputs["k_norm"] = gain(HEAD_DIM)
    inputs["w_hy_out"] = normal((DEPTH, HY_WIDTH, D_MODEL), HY_WIDTH ** -0.5)
    inputs["w_att_out"] = normal((DEPTH, ATT_Q, D_MODEL), ATT_Q ** -0.5)
    inputs["w_out"] = normal((DEPTH, D_MODEL, D_MODEL), D_MODEL ** -0.5)
    inputs["ffn2_norm_pre"] = gain(D_MODEL)
    inputs["ffn2_norm_post"] = gain(D_MODEL)
    inputs["ffn2_w_gate"] = normal((DEPTH, D_MODEL, FF_DIM), D_MODEL ** -0.5)
    inputs["ffn2_w_up"] = normal((DEPTH, D_MODEL, FF_DIM), D_MODEL ** -0.5)
    inputs["ffn2_w_down"] = normal((DEPTH, FF_DIM, D_MODEL), FF_DIM ** -0.5)
    inputs["ple_norm_pre"] = gain(D_MODEL)
    inputs["ple_norm_post"] = gain(D_MODEL)
    inputs["w_ple_gate"] = normal((DEPTH, D_MODEL, D_MODEL), D_MODEL ** -0.5)
    inputs["w_ple_proj"] = normal((DEPTH, PLE_DIM, D_MODEL), PLE_DIM ** -0.5)
    return inputs


def reference(x, p, ffn1_norm_pre, ffn1_norm_post, ffn1_w_gate, ffn1_w_up, ffn1_w_down,
              mix_norm_pre, mix_norm_post, w_in, hy_short_w, hy_short_b,
              filt_w1, filt_b1, filt_freq1, filt_w2, filt_b2, filt_freq2, filt_w3,
              filt_deltas, hy_bias, q_norm, k_norm, w_hy_out, w_att_out, w_out,
              ffn2_norm_pre, ffn2_norm_post, ffn2_w_gate, ffn2_w_up, ffn2_w_down,
              ple_norm_pre, ple_norm_post, w_ple_gate, w_ple_proj):
    L = x.shape[1]
    rope_cos, rope_sin = axial_rope(L)
    for i in range(DEPTH):
        h = swiglu(rms_norm(x, ffn1_norm_pre[i]), ffn1_w_gate[i], ffn1_w_up[i], ffn1_w_down[i])
        x = x + 0.5 * rms_norm(h, ffn1_norm_post[i])

        u = rms_norm(x, mix_norm_pre[i])
        proj = u @ w_in[i]
        hy_in, q, k, v, gate_a, gate_b = jnp.split(proj, SPLITS, axis=-1)

        filt_f = hyena_filter_freq(L, filt_w1[i], filt_b1[i], filt_freq1[i], filt_w2[i],
                                   filt_b2[i], filt_freq2[i], filt_w3[i], filt_deltas[i])
        y_a = hyena_branch(hy_in, hy_short_w[i], hy_short_b[i], filt_f, hy_bias[i])
        y_b = attention_branch(q, k, v, q_norm[i], k_norm[i], rope_cos, rope_sin)

        merged = (jax.nn.sigmoid(gate_a) * (y_a @ w_hy_out[i])
                  + jax.nn.sigmoid(gate_b) * (y_b @ w_att_out[i]))
        x = x + rms_norm(merged @ w_out[i], mix_norm_post[i])

        h = swiglu(rms_norm(x, ffn2_norm_pre[i]), ffn2_w_gate[i], ffn2_w_up[i], ffn2_w_down[i])
        x = x + 0.5 * rms_norm(h, ffn2_norm_post[i])

        g = jax.nn.sigmoid(rms_norm(x, ple_norm_pre[i]) @ w_ple_gate[i])
        x = x + rms_norm(g * (p[i] @ w_ple_proj[i]), ple_norm_post[i])
    return x
```

```python
import math
import contextlib
import numpy as np
import ml_dtypes
import concourse.bass as bass
import concourse.mybir as mybir
from concourse.bass_utils import run_bass_kernel_spmd

F32 = mybir.dt.float32
BF16 = mybir.dt.bfloat16
AF = mybir.ActivationFunctionType
ALU = mybir.AluOpType
AX = mybir.AxisListType

NCORES = 8
L = 4096
NB = 2
NT = NB * L
D = 1024
FF = 2816
HYW = 512
INC = 4352
PLE = 256
EPS = 1e-6
SEM_ROLL = 30000
ARENA_WORDS = 52480


class Res:
    __slots__ = ("last_w", "reads")

    def __init__(self):
        self.last_w = None
        self.reads = []


class Tile:
    def __init__(self, ap, slot):
        self.ap = ap
        self.r = Res()
        self.slot = slot

    def __getitem__(self, k):
        return self.ap[k]


class Sched:
    ENGS = ("pe", "act", "dve", "pool", "sp")

    def __init__(self, nc):
        self.nc = nc
        self.ops = {e: [] for e in self.ENGS}
        self.cnt = {e: 0 for e in self.ENGS}
        self.epoch = {e: 0 for e in self.ENGS}
        self.known = {e: {} for e in self.ENGS}
        self.semkeys = []
        self.dma_slots = {}
        self.total = {}
        for e in self.ENGS:
            key = ("eng", e, 0)
            self.semkeys.append(key)
            self.total[key] = 0

    def _eng_token(self, e):
        if self.cnt[e] >= SEM_ROLL:
            self.epoch[e] += 1
            self.cnt[e] = 0
            self.semkeys.append(("eng", e, self.epoch[e]))
        self.cnt[e] += 1
        key = ("eng", e, self.epoch[e])
        self.total[key] = self.cnt[e]
        return (key, self.cnt[e], e)

    def _dma_token(self, slot):
        st = self.dma_slots.get(slot)
        if st is None or st[1] + 16 > SEM_ROLL:
            ep = 0 if st is None else st[2] + 1
            key = ("dma", slot, ep)
            self.semkeys.append(key)
            st = [key, 0, ep]
            self.dma_slots[slot] = st
        st[1] += 16
        self.total[st[0]] = st[1]
        return (st[0], st[1], "dma")

    def _need(self, e, tok, skip_same):
        if tok is None:
            return
        key, val, src = tok
        if src == e and (skip_same or e == "pe"):
            return
        if key[0] == "dma":
            val = max(val, self.total.get(key, 0))
        k = self.known[e]
        if k.get(key, 0) >= val:
            return
        k[key] = val
        self.ops[e].append(("wait", key, val))

    def _deps(self, e, reads, writes):
        for t in reads:
            self._need(e, t.r.last_w, False)
        for t in writes:
            self._need(e, t.r.last_w, False)
            for tok in t.r.reads:
                self._need(e, tok, False)

    def _commit(self, tok, reads, writes):
        for t in reads:
            rr = t.r.reads
            rr.append(tok)
            if len(rr) > 48:
                best = {}
                for x in rr:
                    if x[0] not in best or best[x[0]][1] < x[1]:
                        best[x[0]] = x
                t.r.reads = list(best.values())
        for t in writes:
            t.r.last_w = tok
            t.r.reads = []

    def op(self, e, fn, rd=(), wr=()):
        self._deps(e, rd, wr)
        tok = self._eng_token(e)
        self.ops[e].append(("op", fn, tok[0], 1))
        self._commit(tok, rd, wr)

    def dma(self, e, slot, fn, rd=(), wr=()):
        self._deps(e, rd, wr)
        tok = self._dma_token(slot)
        self.ops[e].append(("op", fn, tok[0], 16))
        self._commit(tok, rd, wr)

    def barrier(self):
        for e in self.ENGS:
            for key in list(self.semkeys):
                v = self.total.get(key, 0)
                if v <= 0:
                    continue
                k = self.known[e]
                if k.get(key, 0) >= v:
                    continue
                k[key] = v
                self.ops[e].append(("wait", key, v))

    def emit(self):
        nc = self.nc
        with contextlib.ExitStack() as st:
            sems = {}
            assert len(self.semkeys) <= 96, ("too many semaphores", len(self.semkeys))
            for i, key in enumerate(self.semkeys):
                sems[key] = st.enter_context(nc.semaphore("s%d" % i))
            block = st.enter_context(nc.Block())

            def run(eng, lst):
                for it in lst:
                    if it[0] == "wait":
                        eng.wait_ge(sems[it[1]], it[2])
                    else:
                        it[1](eng).then_inc(sems[it[2]], it[3])

            ops = self.ops

            @block.tensor
            def _(eng):
                run(eng, ops["pe"])

            @block.scalar
            def _(eng):
                run(eng, ops["act"])

            @block.vector
            def _(eng):
                run(eng, ops["dve"])

            @block.gpsimd
            def _(eng):
                run(eng, ops["pool"])

            @block.sync
            def _(eng):
                run(eng, ops["sp"])


class KB:
    def __init__(self, nc, arena, banks):
        self.nc = nc
        self.S = Sched(nc)
        self.arena = arena
        self.banks = banks
        self.off = 0
        self.nslot = 0
        self.bank_tiles = [Tile(b, None) for b in banks]

    def reset(self):
        self.S.barrier()
        self.off = 0
        self.nslot = 0
        for t in self.bank_tiles:
            t.r = Res()
        self.cst = self.alloc([128, 2], F32)
        self.memset(self.cst[:, 0:1], EPS, (self.cst,))
        self.memset(self.cst[:, 1:2], 1.0, (self.cst,))

    def alloc(self, shape, dt, parts=128):
        n = 1
        for s in shape[1:]:
            n *= s
        words = (n * (2 if dt == BF16 else 4) + 3) // 4
        words = (words + 7) // 8 * 8
        assert self.off + words <= ARENA_WORDS, ("arena overflow", self.off, words)
        ap = self.arena[0:shape[0], self.off:self.off + words]
        self.off += words
        if dt == BF16:
            ap = ap.bitcast(BF16)
        ap = ap[:, 0:n]
        if len(shape) == 3:
            ap = ap.rearrange("p (a b) -> p a b", a=shape[1])
        elif len(shape) == 4:
            ap = ap.rearrange("p (a b c) -> p a b c", a=shape[1], b=shape[2])
        t = Tile(ap, "d%d" % self.nslot)
        self.nslot += 1
        return t

    def ring(self, n, shape, dt):
        return [self.alloc(shape, dt) for _ in range(n)]

    def bank(self, i):
        return self.bank_tiles[i]

    def dma(self, q, tile, out, in_, rd=(), wr=()):
        self.S.dma(q, tile.slot + ("p" if q == "pool" else "s"), lambda e: e.dma_start(out=out, in_=in_), rd, wr)

    def load(self, q, tile, out, in_):
        self.dma(q, tile, out, in_, (), (tile,))

    def store(self, q, tile, out, in_):
        self.dma(q, tile, out, in_, (tile,), ())

    def mm(self, out, lhsT, rhs, start, stop, rd, wr):
        self.S.op("pe", lambda e: e.matmul(out, lhsT=lhsT, rhs=rhs, start=start, stop=stop), rd, wr)

    def tr(self, out, in_, ident, rd, wr):
        self.S.op("pe", lambda e: e.transpose(out=out, in_=in_, identity=ident), rd, wr)

    def act(self, out, in_, func, rd, wr, **kw):
        self.S.op("act", lambda e: e.activation(out=out, in_=in_, func=func, **kw), rd, wr)

    def tt(self, out, a, b, op, rd, wr, eng="dve"):
        self.S.op(eng, lambda e: e.tensor_tensor(out=out, in0=a, in1=b, op=op), rd, wr)

    def ts(self, out, a, s1, s2, op0, op1, rd, wr, eng="dve"):
        if s2 is None:
            self.S.op(eng, lambda e: e.tensor_scalar(out=out, in0=a, scalar1=s1, scalar2=None, op0=op0), rd, wr)
        else:
            self.S.op(eng, lambda e: e.tensor_scalar(out=out, in0=a, scalar1=s1, scalar2=s2, op0=op0, op1=op1), rd, wr)

    def stt(self, out, a, s, b, op0, op1, rd, wr, eng="dve"):
        self.S.op(eng, lambda e: e.scalar_tensor_tensor(out=out, in0=a, scalar=s, in1=b, op0=op0, op1=op1), rd, wr)

    def cp(self, out, in_, rd, wr, eng="dve"):
        self.S.op(eng, lambda e: e.tensor_copy(out=out, in_=in_), rd, wr)

    def recip(self, out, in_, rd, wr):
        self.S.op("dve", lambda e: e.reciprocal(out=out, in_=in_), rd, wr)

    def memset(self, ap, val, wr, eng="dve"):
        self.S.op(eng, lambda e: e.memset(ap, val), (), wr)

    def reduce(self, out, in_, rd, wr):
        self.S.op("dve", lambda e: e.tensor_reduce(out=out, in_=in_, axis=AX.X, op=ALU.add), rd, wr)

    def rstd_from_ssq(self, st, n):
        self.act(st[:, 1:2], st[:, 0:1], AF.Ln, (st, self.cst), (st,), scale=1.0 / n, bias=self.cst[:, 0:1])
        self.act(st[:, 1:2], st[:, 1:2], AF.Exp, (st,), (st,), scale=-0.5)

    def sigmoid(self, out, in_, rd, wr, tmp):
        self.act(tmp[:], in_, AF.Exp, rd, (tmp,), scale=-1.0)
        self.act(tmp[:], tmp[:], AF.Ln, (tmp, self.cst), (tmp,), bias=self.cst[:, 1:2])
        self.act(out, tmp[:], AF.Exp, (tmp,), wr, scale=-1.0)


def load_ident(K, io):
    idf = K.alloc([128, 128], F32)
    idb = K.alloc([128, 128], BF16)
    K.load("sp", idf, idf[:], io["ident"])
    K.cp(idb[:], idf[:], (idf,), (idb,))
    return idb


def norm_a(K, xt, st, xn, junk, eng="act"):
    K.act(junk[:], xt[:], AF.Square, (xt,), (junk, st), accum_out=st[:, 0:1])
    K.rstd_from_ssq(st, D)
    if eng == "act":
        K.act(xn[:], xt[:], AF.Copy, (xt, st), (xn,), scale=st[:, 1:2])
    else:
        K.ts(xn[:], xt[:], st[:, 1:2], None, ALU.mult, None, (xt, st), (xn,))


def norm_b(K, xn, idb, xnT_dst, dst_tile, pT, eng="dve"):
    pTv = pT.ap.bitcast(BF16).rearrange("p (a b) -> p a b", a=8)
    for kc in range(8):
        K.tr(pTv[:, kc, :], xn[:, kc * 128:(kc + 1) * 128], idb[:], (xn, idb), (pT,))
    if eng == "dve":
        K.cp(xnT_dst, pTv, (pT,), (dst_tile,))
    else:
        K.act(xnT_dst, pTv, AF.Copy, (pT,), (dst_tile,))


def norm_transpose(K, xt, idb, xnT_dst, dst_tile, st, xn, junk, pT, gbc=None):
    norm_a(K, xt, st, xn, junk)
    norm_b(K, xn, idb, xnT_dst, dst_tile, pT)


def post_norm_residual(K, po, xt, gbc, st, junk, outt, scale_rows=None):
    K.act(junk[:, 0:512], po[0][:], AF.Square, (po[0],), (junk, st), accum_out=st[:, 2:3])
    K.act(junk[:, 512:1024], po[1][:], AF.Square, (po[1],), (junk, st), accum_out=st[:, 3:4])
    K.tt(st[:, 0:1], st[:, 2:3], st[:, 3:4], ALU.add, (st,), (st,))
    K.rstd_from_ssq(st, D)
    for dh in range(2):
        sl = slice(dh * 512, (dh + 1) * 512)
        K.stt(outt[:, sl], po[dh][:], st[:, 1:2], gbc[:, sl], ALU.mult, ALU.mult, (po[dh], st, gbc), (outt,))
    K.tt(outt[:], outt[:], xt[:], ALU.add, (outt, xt), (outt,))


def load_weight_bf16(K, wt, w_dram, nk, gcol=None):
    for kc in range(nk):
        K.load("pool", wt, wt[:, kc, :], w_dram[kc * 128:(kc + 1) * 128, :])
    if gcol is not None:
        for kc in range(nk):
            K.ts(wt[:, kc, :], wt[:, kc, :], gcol[:, kc:kc + 1], None, ALU.mult, None, (wt, gcol), (wt,))


def ffn_weights(K, io, pfx, load=("g", "u", "d"), alloc_d=True):
    gpre = K.alloc([128, 8], F32)
    Wg = K.alloc([128, 8, FF], BF16)
    Wu = K.alloc([128, 8, FF], BF16)
    Wd = K.alloc([128, 22, D], BF16) if alloc_d else None
    if "g" in load:
        K.load("sp", gpre, gpre[:], io[pfx + "_norm_pre"])
        load_weight_bf16(K, Wg, io[pfx + "_w_gate"], 8, gpre)
    if "u" in load:
        load_weight_bf16(K, Wu, io[pfx + "_w_up"], 8, gpre)
    if "d" in load:
        load_weight_bf16(K, Wd, io[pfx + "_w_down"], 22)
    return Wg, Wu, Wd


def phase_ffn(K, io, xin, xout, pfx, preloaded=False):
    K.reset()
    Wg, Wu, Wd = ffn_weights(K, io, pfx, load=(("d",) if preloaded else ("g", "u", "d")))
    idb = load_ident(K, io)
    gbc = K.alloc([128, D], F32)
    K.load("sp", gbc, gbc[:], io[pfx + "_norm_post"].partition_broadcast(128))
    K.ts(gbc[:], gbc[:], 0.5, None, ALU.mult, None, (gbc,), (gbc,))
    xas = K.ring(2, [128, D], F32)
    xbs = K.ring(2, [128, D], F32)
    xnT = K.alloc([128, 8, 512], BF16)
    hT = K.alloc([128, 22, 512], BF16)
    xns = K.ring(2, [128, D], BF16)
    junk = K.alloc([128, D], BF16)
    sls = K.ring(2, [128, 512], F32)
    outs = K.ring(2, [128, D], F32)
    sts = K.ring(4, [128, 4], F32)
    st2 = K.ring(2, [128, 4], F32)
    pT = K.bank(0)
    pG = [K.bank(1), K.bank(2)]
    pU = [K.bank(3), K.bank(4)]
    pOs = [[K.bank(5), K.bank(6)], [K.bank(7), K.bank(4)]]
    nblk = NT // 512
    cnt = {"p": 0}

    def prep(blk, tt):
        t0 = blk * 512 + tt * 128
        i = cnt["p"]
        cnt["p"] += 1
        xa = xas[i % 2]
        K.load("sp", xa, xa[:], xin[t0:t0 + 128, :])
        norm_transpose(K, xa, idb, xnT[:, :, tt * 128:(tt + 1) * 128], xnT, sts[i % 4], xns[i % 2], junk, pT)

    for tt in range(4):
        prep(0, tt)
    for blk in range(nblk):
        for f in range(22):
            g = pG[f % 2]
            u = pU[f % 2]
            for kc in range(8):
                K.mm(g[:], Wg[:, kc, f * 128:(f + 1) * 128], xnT[:, kc, :], kc == 0, kc == 7, (Wg, xnT), (g,))
            for kc in range(8):
                K.mm(u[:], Wu[:, kc, f * 128:(f + 1) * 128], xnT[:, kc, :], kc == 0, kc == 7, (Wu, xnT), (u,))
            sl = sls[f % 2]
            K.sigmoid(sl[:], g[:], (g,), (sl,), sl)
            K.tt(sl[:], sl[:], g[:], ALU.mult, (sl, g), (sl,))
            K.tt(hT[:, f, :], sl[:], u[:], ALU.mult, (sl, u), (hT,))
        for tt in range(4):
            t0 = blk * 512 + tt * 128
            pO = pOs[tt % 2]
            xb = xbs[tt % 2]
            K.load("sp", xb, xb[:], xin[t0:t0 + 128, :])
            for dh in range(2):
                for f in range(22):
                    K.mm(pO[dh][:], hT[:, f, tt * 128:(tt + 1) * 128], Wd[:, f, dh * 512:(dh + 1) * 512],
                         f == 0, f == 21, (hT, Wd), (pO[dh],))
            if blk + 1 < nblk:
                prep(blk + 1, tt)
            outt = outs[tt % 2]
            post_norm_residual(K, pO, xb, gbc, st2[tt % 2], junk, outt)
            K.store("pool", outt, xout[t0:t0 + 128, :], outt[:])


def headnorm_rope(K, src_ps, src_tile, nh, g_bc, cosj, sinj, tabs, w, out_bf, out_tile):
    sq, st8, qn, t0, t1 = w
    n = nh * 64
    K.act(sq[:, 0:n], src_ps, AF.Square, (src_tile,), (sq,))
    K.reduce(st8[:, 0:nh], sq[:, 0:n].rearrange("p (h d) -> p h d", h=nh), (sq,), (st8,))
    K.act(st8[:, 8:8 + nh], st8[:, 0:nh], AF.Ln, (st8, K.cst), (st8,), scale=1.0 / 64, bias=K.cst[:, 0:1])
    K.act(st8[:, 8:8 + nh], st8[:, 8:8 + nh], AF.Exp, (st8,), (st8,), scale=-0.5)
    qv = qn[:, 0:n].rearrange("p (h d) -> p h d", h=nh)
    K.tt(qv, src_ps.rearrange("p (h d) -> p h d", h=nh), st8[:, 8:8 + nh].unsqueeze(2).to_broadcast([128, nh, 64]),
         ALU.mult, (src_tile, st8), (qn,))
    K.tt(qv, qv, g_bc[:, :].unsqueeze(1).to_broadcast([128, nh, 64]), ALU.mult, (qn, g_bc), (qn,))
    x0 = qv[:, :, 0:64:2]
    x1 = qv[:, :, 1:64:2]
    cb = cosj.unsqueeze(1).to_broadcast([128, nh, 32])
    sb = sinj.unsqueeze(1).to_broadcast([128, nh, 32])
    a = t0[:, 0:nh * 32].rearrange("p (h d) -> p h d", h=nh)
    b = t1[:, 0:nh * 32].rearrange("p (h d) -> p h d", h=nh)
    K.tt(a, x0, cb, ALU.mult, (qn, tabs), (t0,))
    K.tt(b, x1, sb, ALU.mult, (qn, tabs), (t1,))
    K.tt(out_bf[:, :, 0:64:2], a, b, ALU.subtract, (t0, t1), (out_tile,))
    K.tt(a, x0, sb, ALU.mult, (qn, tabs), (t0,))
    K.tt(b, x1, cb, ALU.mult, (qn, tabs), (t1,))
    K.tt(out_bf[:, :, 1:64:2], a, b, ALU.add, (t0, t1), (out_tile,))


def phase_win(K, io, sc):
    K.reset()
    idb = load_ident(K, io)
    gpre = K.alloc([128, 8], F32)
    K.load("sp", gpre, gpre[:], io["mix_norm_pre"])
    Win = K.alloc([128, 8, INC], BF16)
    load_weight_bf16(K, Win, io["w_in"], 8, gpre)
    gq = K.alloc([128, 64], F32)
    gk = K.alloc([128, 64], F32)
    K.load("sp", gq, gq[:], io["q_norm"].partition_broadcast(128))
    K.load("sp", gk, gk[:], io["k_norm"].partition_broadcast(128))
    tabs = K.alloc([128, 2, 32, 32], F32)
    K.load("sp", tabs, tabs[:, 0, :, :], io["rope_cos"].rearrange("p (j i) -> p j i", j=32))
    K.load("sp", tabs, tabs[:, 1, :, :], io["rope_sin"].rearrange("p (j i) -> p j i", j=32))
    xts = K.ring(2, [128, D], F32)
    xns = K.ring(2, [128, D], BF16)
    junk = K.alloc([128, D], BF16)
    uTs = K.ring(2, [128, 8, 128], BF16)
    sts = K.ring(2, [128, 4], F32)
    hyts = K.ring(2, [128, 1536], F32)
    sgts = K.ring(2, [128, 2048], F32)
    sgtmp = K.ring(2, [128, 512], F32)
    w = (K.alloc([128, 512], F32), K.alloc([128, 16], F32), K.alloc([128, 512], F32),
         K.alloc([128, 256], F32), K.alloc([128, 256], F32))
    qrs = K.ring(2, [128, 8, 64], BF16)
    krs = K.ring(2, [128, 2, 64], BF16)
    qTs = K.ring(2, [64, 8, 128], BF16)
    kTs = K.ring(2, [64, 2, 128], BF16)
    vas = K.ring(2, [128, 2, 68], BF16)
    for va in vas:
        K.memset(va[:], 0.0, (va,))
        K.memset(va[:, :, 64:65], 1.0, (va,))
    pT = K.bank(0)
    pQ = K.bank(7)
    ring = [K.bank(i) for i in range(1, 7)]
    rc = 0
    chunks = [(0, 512), (512, 512), (1024, 512), (1536, 512), (2048, 256),
              (2304, 512), (2816, 512), (3328, 512), (3840, 512)]
    def prep_a(ti):
        xt = xts[ti % 2]
        K.load("sp", xt, xt[:], sc["x1"][ti * 128:(ti + 1) * 128, :])
        norm_a(K, xt, sts[ti % 2], xns[ti % 2], junk)

    def prep_b(ti):
        norm_b(K, xns[ti % 2], idb, uTs[ti % 2][:], uTs[ti % 2], pT)

    prep_a(0)
    prep_b(0)
    order = [3, 4, 0, 1, 2, 5, 6, 7, 8]
    for ti in range(NT // 128):
        b, j = divmod(ti, 32)
        t0 = ti * 128
        uT = uTs[ti % 2]
        if ti + 1 < NT // 128:
            prep_a(ti + 1)
        hyt = hyts[ti % 2]
        sgt = sgts[ti % 2]
        qr = qrs[ti % 2]
        kr = krs[ti % 2]
        for ci in order:
            c0, cw = chunks[ci]
            pb = ring[rc % 6]
            rc += 1
            for kc in range(8):
                K.mm(pb[:, 0:cw], uT[:, kc, :], Win[:, kc, c0:c0 + cw], kc == 0, kc == 7, (uT, Win), (pb,))
            if ci < 3:
                K.cp(hyt[:, c0:c0 + 512], pb[:], (pb,), (hyt,))
            elif ci == 3:
                headnorm_rope(K, pb[:], pb, 8, gq, tabs[:, 0, j, :], tabs[:, 1, j, :], tabs, w, qr[:], qr)
            elif ci == 4:
                headnorm_rope(K, pb[:, 0:128], pb, 2, gk, tabs[:, 0, j, :], tabs[:, 1, j, :], tabs, w, kr[:], kr)
                va = vas[ti % 2]
                K.cp(va[:, :, 0:64], pb[:, 128:256].rearrange("p (g d) -> p g d", g=2), (pb,), (va,))
                K.store("pool", va, sc["VA"][b].rearrange("g t c -> t g c")[j * 128:(j + 1) * 128, :, :], va[:])
            else:
                K.act(sgt[:, c0 - 2304:c0 - 2304 + 512], pb[:], AF.Copy, (pb,), (sgt,))
        if ti + 1 < NT // 128:
            prep_b(ti + 1)
        pQv = pQ.ap.bitcast(BF16)
        for h in range(8):
            K.tr(pQv[0:64, h * 128:(h + 1) * 128], qr[:, h, :], idb[:], (qr, idb), (pQ,))
        qT = qTs[ti % 2]
        K.cp(qT[:], pQv[0:64, :].rearrange("p (h t) -> p h t", h=8), (pQ,), (qT,))
        K.store("pool", qT, sc["QT"][b].rearrange("h d t -> d h t")[:, :, j * 128:(j + 1) * 128], qT[:])
        pKb = ring[rc % 6]
        rc += 1
        pKv = pKb.ap.bitcast(BF16)
        for g in range(2):
            K.tr(pKv[0:64, g * 128:(g + 1) * 128], kr[:, g, :], idb[:], (kr, idb), (pKb,))
        kT = kTs[ti % 2]
        K.cp(kT[:], pKv[0:64, 0:256].rearrange("p (g t) -> p g t", g=2), (pKb,), (kT,))
        K.store("pool", kT, sc["KT"][b].rearrange("g d t -> d g t")[:, :, j * 128:(j + 1) * 128], kT[:])
        K.store("pool", hyt, sc["hy"][t0:t0 + 128, :], hyt[:])
        K.store("pool", sgt, sc["SG"][t0:t0 + 128, :], sgt[:])


def phase_attn(K, io, sc):
    K.reset()
    idf = K.alloc([128, 128], F32)
    K.load("sp", idf, idf[:], io["ident"])
    gq = K.alloc([128, 64], F32)
    gk = K.alloc([128, 64], F32)
    K.load("sp", gq, gq[:], io["q_norm"].partition_broadcast(128))
    K.load("sp", gk, gk[:], io["k_norm"].partition_broadcast(128))
    mx = K.alloc([128, 4], F32)
    K.S.op("dve", lambda e: e.tensor_reduce(out=mx[:, 0:1], in_=gq[:], axis=AX.X, op=ALU.max, apply_absolute_value=True), (gq,), (mx,))
    K.S.op("dve", lambda e: e.tensor_reduce(out=mx[:, 1:2], in_=gk[:], axis=AX.X, op=ALU.max, apply_absolute_value=True), (gk,), (mx,))
    K.tt(mx[:, 2:3], mx[:, 0:1], mx[:, 1:2], ALU.mult, (mx,), (mx,))
    K.ts(mx[:, 3:4], mx[:, 2:3], -8.0, None, ALU.mult, None, (mx,), (mx,))
    KTs = K.ring(2, [128, 2, L], BF16)
    VAs = K.ring(2, [128, 2, 32, 68], BF16)
    QTs = K.ring(2, [128, 8, L], BF16)
    PTs = K.ring(5, [128, 1024], BF16)
    oTs = K.ring(2, [68, 512], F32)
    ybts = K.ring(2, [128, 4, 512], BF16)
    rdens = K.ring(2, [128, 4], F32)
    pOT = [K.bank(4), K.bank(5)]
    pTr = [K.bank(6), K.bank(7)]
    for b in range(NB):
        K.memset(KTs[b][64:128, :, :], 0.0, (KTs[b],))
        K.memset(QTs[b][64:128, :, :], 0.0, (QTs[b],))
        for g in range(2):
            K.load("sp", KTs[b], KTs[b][0:64, g, :], sc["KT"][b, g])
            K.load("sp", VAs[b], VAs[b][:, g, :, :], sc["VA"][b, g].rearrange("(kc p) c -> p kc c", p=128))
        for h in range(8):
            K.load("sp", QTs[b], QTs[b][0:64, h, :], sc["QT"][b, h])
    items = [(b, qb, h) for b in range(NB) for qb in range(L // 512) for h in range(8)]
    state = {"sc": 0, "fin": []}

    def s_pair(it, kp):
        b, qb, h = it
        g = h // 4
        i = state["sc"]
        state["sc"] += 1
        b0 = 2 * (i % 2)
        pt = PTs[i % 5]
        for u in range(2):
            kc = 2 * kp + u
            ps = K.bank(b0 + u)
            K.mm(ps[:], KTs[b][:, g, kc * 128:(kc + 1) * 128], QTs[b][:, h, qb * 512:(qb + 1) * 512], True, True, (KTs[b], QTs[b]), (ps,))
        K.act(pt[:], K.psum[:, b0 * 512:(b0 + 2) * 512], AF.Exp, (K.bank(b0), K.bank(b0 + 1), mx), (pt,), scale=0.125, bias=mx[:, 3:4])
        return pt

    def pv_pair(pi, kp, pt):
        pb, pqb, ph = items[pi]
        for u in range(2):
            kc = 2 * kp + u
            K.mm(pOT[pi % 2][0:68, :], VAs[pb][:, ph // 4, kc, :], pt[:, u * 512:(u + 1) * 512], kc == 0, kc == 31, (VAs[pb], pt), (pOT[pi % 2],))
        if kp == 15:
            state["fin"].append(pi)

    def finish(idx):
        b, qb, h = items[idx]
        po, oT, ptr, rden, ybt = pOT[idx % 2], oTs[idx % 2], pTr[idx % 2], rdens[idx % 2], ybts[(idx // 8) % 2]
        K.cp(oT[:], po[0:68, :], (po,), (oT,))
        for qt in range(4):
            K.tr(ptr[:, qt * 68:(qt + 1) * 68], oT[:, qt * 128:(qt + 1) * 128], idf[0:68, 0:68], (oT, idf), (ptr,))
        pv = ptr[:, 0:272].rearrange("p (q c) -> p q c", q=4)
        K.recip(rden[:].unsqueeze(2), pv[:, :, 64:65], (ptr,), (rden,))
        K.tt(ybt[:, :, h * 64:(h + 1) * 64], pv[:, :, 0:64], rden[:].unsqueeze(2).to_broadcast([128, 4, 64]), ALU.mult, (ptr, rden), (ybt,))
        if h == 7:
            t0 = b * L + qb * 512
            K.store("pool", ybt, sc["yb"][t0:t0 + 512, :].rearrange("(q p) c -> p q c", p=128), ybt[:])

    pend = []
    for idx, it in enumerate(items):
        for kp in range(16):
            pt = s_pair(it, kp)
            pend.append((idx, kp, pt))
            if len(pend) > 2:
                pv_pair(*pend.pop(0))
            if kp == 4 and state["fin"]:
                finish(state["fin"].pop(0))
    while pend:
        pv_pair(*pend.pop(0))
    while state["fin"]:
        finish(state["fin"].pop(0))


def transpose_bf(K, src, src_tile, n, idb, pT, dst, dst_tile, eng="dve"):
    pTv = pT.ap.bitcast(BF16)[:, 0:n * 128].rearrange("p (a b) -> p a b", a=n)
    for kc in range(n):
        K.tr(pTv[:, kc, :], src[:, kc * 128:(kc + 1) * 128], idb[:], (src_tile, idb), (pT,))
    if eng == "dve":
        K.cp(dst, pTv, (pT,), (dst_tile,))
    else:
        K.act(dst, pTv, AF.Copy, (pT,), (dst_tile,))


def phase_merge(K, io, sc, prefetch_ffn2=True):
    K.reset()
    if prefetch_ffn2:
        ffn_weights(K, io, "ffn2", load=("g", "u"), alloc_d=False)
    idb = load_ident(K, io)
    gbc = K.alloc([128, D], F32)
    K.load("sp", gbc, gbc[:], io["mix_norm_post"].partition_broadcast(128))
    Wa = K.alloc([128, 4, D], BF16)
    Wb = K.alloc([128, 4, D], BF16)
    Wo = K.alloc([128, 8, D], BF16)
    load_weight_bf16(K, Wa, io["w_hy_out"], 4)
    load_weight_bf16(K, Wb, io["w_att_out"], 4)
    load_weight_bf16(K, Wo, io["w_out"], 8)
    xts = K.ring(3, [128, D], F32)
    yas = K.ring(2, [128, 512], BF16)
    ybs = K.ring(2, [128, 512], BF16)
    sgs = K.ring(2, [128, 2048], F32)
    yaTs = K.ring(2, [128, 4, 128], BF16)
    ybTs = K.ring(2, [128, 4, 128], BF16)
    m1s = K.ring(2, [128, 512], F32)
    m2s = K.ring(2, [128, 512], F32)
    mts = K.ring(3, [128, D], BF16)
    mTs = K.ring(3, [128, 8, 128], BF16)
    outs = K.ring(3, [128, D], F32)
    sts = K.ring(2, [128, 4], F32)
    junk = K.alloc([128, D], BF16)
    pT = [K.bank(0), K.bank(7)]
    pA = [K.bank(1), K.bank(2)]
    pB = [K.bank(3), K.bank(4)]
    pO = [K.bank(5), K.bank(6)]
    ntile = NT // 128

    def prep(ti):
        t0 = ti * 128
        r = ti % 2
        ya, yb, sg = yas[r], ybs[r], sgs[r]
        K.load("sp", ya, ya[:], sc["ya"][t0:t0 + 128, :])
        K.load("sp", yb, yb[:], sc["yb"][t0:t0 + 128, :])
        K.load("sp", sg, sg[:], sc["SG"][t0:t0 + 128, :])
        transpose_bf(K, ya, ya, 4, idb, pT[0], yaTs[r][:], yaTs[r], eng="act")
        transpose_bf(K, yb, yb, 4, idb, pT[1], ybTs[r][:], ybTs[r], eng="act")

    def stage1(ti):
        r = ti % 2
        sg, mt = sgs[r], mts[ti % 3]
        for dh in range(2):
            sl = slice(dh * 512, (dh + 1) * 512)
            for kc in range(4):
                K.mm(pA[dh][:], yaTs[r][:, kc, :], Wa[:, kc, sl], kc == 0, kc == 3, (yaTs[r], Wa), (pA[dh],))
            for kc in range(4):
                K.mm(pB[dh][:], ybTs[r][:, kc, :], Wb[:, kc, sl], kc == 0, kc == 3, (ybTs[r], Wb), (pB[dh],))
            m1, m2 = m1s[dh], m2s[dh]
            K.tt(m1[:], sg[:, dh * 512:(dh + 1) * 512], pA[dh][:], ALU.mult, (sg, pA[dh]), (m1,))
            K.tt(m2[:], sg[:, 1024 + dh * 512:1024 + (dh + 1) * 512], pB[dh][:], ALU.mult, (sg, pB[dh]), (m2,))
            K.tt(mt[:, sl], m1[:], m2[:], ALU.add, (m1, m2), (mt,))

    def stage2(ti):
        t0 = ti * 128
        r = ti % 2
        xt = xts[ti % 3]
        K.load("sp", xt, xt[:], sc["x1"][t0:t0 + 128, :])
        q3 = ti % 3
        transpose_bf(K, mts[q3], mts[q3], 8, idb, pT[0], mTs[q3][:], mTs[q3], eng="act")
        for dh in range(2):
            for kc in range(8):
                K.mm(pO[dh][:], mTs[q3][:, kc, :], Wo[:, kc, dh * 512:(dh + 1) * 512], kc == 0, kc == 7, (mTs[q3], Wo), (pO[dh],))
        outt = outs[ti % 3]
        post_norm_residual(K, pO, xt, gbc, sts[r], junk, outt)
        K.store("pool", outt, sc["x2"][t0:t0 + 128, :], outt[:])

    prep(0)
    for ti in range(ntile):
        if ti >= 2:
            stage2(ti - 2)
        if ti + 1 < ntile:
            prep(ti + 1)
        stage1(ti)
    stage2(ntile - 2)
    stage2(ntile - 1)


def phase_ple(K, io, sc, xin, yout):
    K.reset()
    idb = load_ident(K, io)
    gpre = K.alloc([128, 8], F32)
    K.load("sp", gpre, gpre[:], io["ple_norm_pre"])
    gbc = K.alloc([128, D], F32)
    K.load("sp", gbc, gbc[:], io["ple_norm_post"].partition_broadcast(128))
    Wg = K.alloc([128, 8, D], BF16)
    Wp = K.alloc([128, 2, D], BF16)
    load_weight_bf16(K, Wg, io["w_ple_gate"], 8, gpre)
    load_weight_bf16(K, Wp, io["w_ple_proj"], 2)
    xts = K.ring(4, [128, D], F32)
    pts = K.ring(4, [128, PLE], F32)
    pbs = K.ring(2, [128, PLE], BF16)
    xns = K.ring(2, [128, D], BF16)
    xnTs = K.ring(2, [128, 8, 128], BF16)
    pTs = K.ring(2, [128, 2, 128], BF16)
    gs = K.ring(2, [128, D], F32)
    prods = K.ring(2, [128, D], F32)
    outs = K.ring(3, [128, D], F32)
    sts = K.ring(2, [128, 4], F32)
    junk = K.alloc([128, D], BF16)
    pT = [K.bank(0), K.bank(7)]
    pG = [K.bank(1), K.bank(2)]
    pP = [K.bank(3), K.bank(4)]
    st2 = K.ring(2, [128, 4], F32)
    junk2 = K.alloc([128, D], BF16)
    sgtmp = K.ring(2, [128, 512], F32)
    ntile = NT // 128

    def prep_a(ti):
        t0 = ti * 128
        r = ti % 2
        xt, pt, pb = xts[ti % 4], pts[ti % 4], pbs[r]
        K.load("sp", xt, xt[:], xin[t0:t0 + 128, :])
        K.load("sp", pt, pt[:], io["p"][t0:t0 + 128, :])
        norm_a(K, xt, sts[r], xns[r], junk, eng="dve")
        K.cp(pb[:], pt[:], (pt,), (pb,))

    def prep_b(ti):
        r = ti % 2
        norm_b(K, xns[r], idb, xnTs[r][:], xnTs[r], pT[0])
        transpose_bf(K, pbs[r], pbs[r], 2, idb, pT[1], pTs[r][:], pTs[r])

    prep_a(0)
    prep_b(0)
    for ti in range(ntile):
        t0 = ti * 128
        r = ti % 2
        xt = xts[ti % 4]
        g, prod = gs[r], prods[r]
        if ti + 1 < ntile:
            prep_a(ti + 1)
        for dh in range(2):
            sl = slice(dh * 512, (dh + 1) * 512)
            pg_, pp_ = pG[dh], pP[dh]
            for kc in range(8):
                K.mm(pg_[:], xnTs[r][:, kc, :], Wg[:, kc, sl], kc == 0, kc == 7, (xnTs[r], Wg), (pg_,))
            for kc in range(2):
                K.mm(pp_[:], pTs[r][:, kc, :], Wp[:, kc, sl], kc == 0, kc == 1, (pTs[r], Wp), (pp_,))
            K.sigmoid(g[:, sl], pg_[:], (pg_,), (g,), sgtmp[dh])
            K.tt(prod[:, sl], g[:, sl], pp_[:], ALU.mult, (g, pp_), (prod,))
        if ti + 1 < ntile:
            prep_b(ti + 1)
        st = st2[r]
        K.act(junk2[:], prod[:], AF.Square, (prod,), (junk2, st), accum_out=st[:, 0:1])
        K.rstd_from_ssq(st, D)
        outt = outs[ti % 3]
        K.stt(outt[:], prod[:], st[:, 1:2], gbc[:], ALU.mult, ALU.mult, (prod, st, gbc), (outt,))
        K.tt(outt[:], outt[:], xt[:], ALU.add, (outt, xt), (outt,))
        K.store("pool", outt, yout[t0:t0 + 128, :], outt[:])


TWO_PI = 2.0 * math.pi


def sin_reduced(K, tmp, ki, kf, dst, dst_tile):
    K.ts(ki[:], tmp[:], 1.0 / TWO_PI, None, ALU.mult, None, (tmp,), (ki,))
    K.cp(kf[:], ki[:], (ki,), (kf,))
    K.stt(tmp[:], kf[:], -TWO_PI, tmp[:], ALU.mult, ALU.add, (kf, tmp), (tmp,))
    K.ts(tmp[:], tmp[:], -3.1415925, 3.1415925, ALU.max, ALU.min, (tmp,), (tmp,))
    K.act(dst, tmp[:], AF.Sin, (tmp,), (dst_tile,))


def phase_filt1(K, io, sc):
    K.reset()
    zT = K.alloc([33, L], F32)
    K.load("sp", zT, zT[:], io["filt_zT"])
    w1 = K.alloc([33, 64], F32)
    w2 = K.alloc([64, 64], F32)
    w3 = K.alloc([64, 2048], F32)
    K.load("sp", w1, w1[:], io["filt_w1"])
    K.load("sp", w2, w2[:], io["filt_w2"])
    K.load("sp", w3, w3[:], io["filt_w3"])
    cols = K.alloc([64, 4], F32)
    for i, nm in enumerate(("filt_b1", "filt_freq1", "filt_b2", "filt_freq2")):
        K.load("sp", cols, cols[:, i:i + 1], io[nm])
    negt = K.alloc([128, 32], F32)
    K.load("sp", negt, negt[:], io["negt"])
    absd = K.alloc([128, 2048], F32)
    K.load("sp", absd, absd[:], io["filt_deltas"].partition_broadcast(128))
    K.stt(absd[:], absd[:], -1.0, absd[:], ALU.mult, ALU.max, (absd,), (absd,))
    ones = K.alloc([128, 128], F32)
    K.memset(ones[:], 1.0, (ones,))
    h1T = K.alloc([64, L], F32)
    h2T = K.alloc([64, L], F32)
    tmp = K.alloc([64, 512], F32)
    kf = K.alloc([64, 512], F32)
    ki = Tile(K.alloc([64, 512], F32).ap.bitcast(mybir.dt.int32), None)
    b1c, f1c, b2c, f2c = (cols[:, i:i + 1] for i in range(4))
    for c in range(8):
        ps = K.bank(c % 2)
        K.mm(ps[0:64, :], w1[:, :], zT[:, c * 512:(c + 1) * 512], True, True, (w1, zT), (ps,))
        K.ts(tmp[:], ps[0:64, :], cols[:, 0:1], cols[:, 1:2], ALU.add, ALU.mult, (ps, cols), (tmp,))
        sin_reduced(K, tmp, ki, kf, h1T[:, c * 512:(c + 1) * 512], h1T)
    for c in range(8):
        ps = K.bank(c % 2)
        K.mm(ps[0:64, :], w2[:, :], h1T[:, c * 512:(c + 1) * 512], True, True, (w2, h1T), (ps,))
        K.ts(tmp[:], ps[0:64, :], cols[:, 2:3], cols[:, 3:4], ALU.add, ALU.mult, (ps, cols), (tmp,))
        sin_reduced(K, tmp, ki, kf, h2T[:, c * 512:(c + 1) * 512], h2T)
    decs = K.ring(2, [128, 2048], F32)
    hfs = K.ring(2, [128, 2048], F32)
    As = K.ring(2, [128, 1024], F32)
    a1s = K.ring(2, [128, 512], F32)
    Ets = K.ring(2, [128, 1024], BF16)
    Ots = K.ring(2, [128, 1024], BF16)
    pA = [K.bank(6), K.bank(7)]
    for j in range(32):
        r = j % 2
        dec, hf, A, Et, Ot, a1 = decs[r], hfs[r], As[r], Ets[r], Ots[r], a1s[r]
        K.act(dec[:], absd[:], AF.Exp, (absd, negt), (dec,), scale=negt[:, j:j + 1])
        for cc in range(4):
            ps = K.bank(2 + cc)
            K.mm(ps[:], h2T[:, j * 128:(j + 1) * 128], w3[:, cc * 512:(cc + 1) * 512], True, True, (h2T, w3), (ps,))
            K.tt(hf[:, cc * 512:(cc + 1) * 512], ps[:], dec[:, cc * 512:(cc + 1) * 512], ALU.mult, (ps, dec), (hf,))
        for o in range(2):
            f_ = hf[:, o * 1024:o * 1024 + 512]
            b_ = hf[:, o * 1024 + 512:o * 1024 + 1024]
            osl = slice(o * 512, (o + 1) * 512)
            K.tt(Et[:, osl], f_, b_, ALU.add, (hf,), (Et,))
            K.tt(Ot[:, osl], f_, b_, ALU.subtract, (hf,), (Ot,))
            K.act(a1[:], f_, AF.Abs, (hf,), (a1,))
            K.act(A[:, osl], b_, AF.Abs, (hf,), (A,))
            K.tt(A[:, osl], A[:, osl], a1[:], ALU.add, (A, a1), (A,))
            if j == 0:
                K.tt(A[0:1, osl], f_[0:1, :], b_[0:1, :], ALU.add, (hf, A), (A,))
                K.stt(A[0:1, osl], A[0:1, osl], -1.0, A[0:1, osl], ALU.mult, ALU.max, (A,), (A,))
        for o in range(2):
            K.mm(pA[o][:], ones[:], A[:, o * 512:(o + 1) * 512], j == 0, j == 31, (ones, A), (pA[o],))
        K.store("pool", Et, sc["Ebf"][j * 128:(j + 1) * 128, :], Et[:])
        K.store("pool", Ot, sc["Obf"][j * 128:(j + 1) * 128, :], Ot[:])
    rn = K.alloc([128, 1024], F32)
    for o in range(2):
        K.recip(rn[:, o * 512:(o + 1) * 512], pA[o][:], (pA[o],), (rn,))
    K.store("pool", rn, sc["rn"], rn[:])


def fold_inplace(K, X, ncol, Rb, Eb_, banks):
    bi = 0
    for j in range(16):
        for h in range(ncol // 512):
            cs = slice(h * 512, (h + 1) * 512)
            pb = banks[bi % len(banks)]
            bi += 1
            K.mm(pb[:], Rb[:], X[:, 31 - j, cs], True, j == 0, (Rb, X), (pb,))
            if j > 0:
                K.mm(pb[:], Eb_[:], X[:, 32 - j, cs], False, True, (Eb_, X), (pb,))
            K.tt(X[:, 32 - j, cs], X[:, j, cs], pb[:], ALU.subtract, (X, pb), (X,))
            K.tt(X[:, j, cs], X[:, j, cs], pb[:], ALU.add, (X, pb), (X,))
    for h in range(ncol // 512):
        cs = slice(h * 512, (h + 1) * 512)
        pb = banks[bi % len(banks)]
        bi += 1
        K.mm(pb[:], Eb_[:], X[:, 16, cs], True, True, (Eb_, X), (pb,))
        K.cp(X[:, 16, cs], pb[:], (pb,), (X,))


def fwd_lists(par):
    e_list = [(j, j) for j in range(17)]
    d_list = [(j, 32 - j) for j in range(16)]
    return (e_list, d_list) if par == 0 else (d_list, e_list)


def load_tab(K, tile, tab, idx, nj):
    K.load("sp", tile, tile[:, 0:nj, :], tab[idx].rearrange("p (j q) -> p j q", j=nj))


def phase_filt2_sconv(K, io, sc):
    K.reset()
    XE = K.alloc([128, 33, 512], BF16)
    XO = K.alloc([128, 33, 512], BF16)
    rn = K.alloc([128, 1024], F32)
    K.load("sp", rn, rn[:], sc["rn"])
    wk = K.alloc([128, 32], F32)
    K.load("sp", wk, wk[:], io["wk"])
    alt = K.alloc([128, 128], BF16)
    K.load("sp", alt, alt[:], io["alt"])
    Rb = K.alloc([128, 128], BF16)
    Eb_ = K.alloc([128, 128], BF16)
    K.load("sp", Rb, Rb[:], io["rev_rb"])
    K.load("sp", Eb_, Eb_[:], io["rev_eb"])
    hns = K.ring(2, [128, 512], F32)
    Cs = K.ring(3, [128, 17, 128], BF16)
    Ss = K.ring(3, [128, 17, 128], BF16)
    Hcs = K.ring(2, [128, 512], F32)
    Hss = K.ring(2, [128, 512], F32)
    wbc = K.alloc([128, 3, 1536], F32)
    bbc = K.alloc([128, 1536], F32)
    K.load("sp", wbc, wbc[:], io["hy_short_w"].partition_broadcast(128).rearrange("p o (k c) -> p (o k) c", k=3))
    K.load("sp", bbc, bbc[:], io["hy_short_b"].partition_broadcast(128))
    a0s = K.ring(2, [128, 1536], F32)
    a1s = K.ring(2, [128, 1536], F32)
    a2s = K.ring(2, [128, 1536], F32)
    hvbs = K.ring(2, [128, 512], BF16)

    def sconv_tile(ti):
        b, j = divmod(ti, 32)
        t0 = ti * 128
        r = ti % 2
        a0, a1, a2, hvb = a0s[r], a1s[r], a2s[r], hvbs[r]
        if j == 0:
            K.memset(a0[:], 0.0, (a0,), eng="pool")
            K.load("sp", a0, a0[1:128, :], sc["hy"][t0:t0 + 127, :])
        else:
            K.load("sp", a0, a0[:], sc["hy"][t0 - 1:t0 + 127, :])
        K.load("sp", a1, a1[:], sc["hy"][t0:t0 + 128, :])
        if j == 31:
            K.memset(a2[:], 0.0, (a2,), eng="pool")
            K.load("sp", a2, a2[0:127, :], sc["hy"][t0 + 1:t0 + 128, :])
        else:
            K.load("sp", a2, a2[:], sc["hy"][t0 + 1:t0 + 129, :])
        K.tt(a0[:], a0[:], wbc[:, 0, :], ALU.mult, (a0, wbc), (a0,), eng="pool")
        K.tt(a2[:], a2[:], wbc[:, 2, :], ALU.mult, (a2, wbc), (a2,), eng="pool")
        K.tt(a1[:], a1[:], wbc[:, 1, :], ALU.mult, (a1, wbc), (a1,))
        K.tt(a1[:], a1[:], bbc[:], ALU.add, (a1, bbc), (a1,))
        K.tt(a1[:], a1[:], a0[:], ALU.add, (a1, a0), (a1,))
        K.tt(a1[:], a1[:], a2[:], ALU.add, (a1, a2), (a1,))
        K.act(hvb[:], a1[:, 0:512], AF.Copy, (a1,), (hvb,))
        K.store("pool", a1, sc["z"][t0:t0 + 128, :], a1[:])
        K.store("pool", hvb, sc["hvb"][t0:t0 + 128, :], hvb[:])

    slc = 0
    for o in range(2):
        osl = slice(o * 512, (o + 1) * 512)
        K.load("sp", XE, XE[:, 0:32, :], sc["Ebf"][:, osl].rearrange("(j p) c -> p j c", p=128))
        K.load("sp", XO, XO[:, 0:32, :], sc["Obf"][:, osl].rearrange("(j p) c -> p j c", p=128))
        ps = K.bank(0)
        hn = hns[o]
        for j in range(32):
            K.mm(ps[:], alt[:], XE[:, j, :], j == 0, j == 31, (alt, XE), (ps,))
        K.stt(hn[:], ps[:], 1.0 / (2 * L), rn[:, osl], ALU.mult, ALU.mult, (ps, rn), (hn,))
        K.store("pool", hn, sc["HN"][:, osl], hn[:])
        fold_inplace(K, XE, 512, Rb, Eb_, [K.bank(0), K.bank(1)])
        fold_inplace(K, XO, 512, Rb, Eb_, [K.bank(0), K.bank(1)])
        for c in range(32):
            r = c % 2
            C_, S_ = Cs[slc % 3], Ss[slc % 3]
            slc += 1
            load_tab(K, C_, io["tfc"], c, 17)
            load_tab(K, S_, io["tfs"], c, 17)
            clist, slist = fwd_lists(c // 16)
            pc = K.bank(2 + r)
            for i, (j, slot) in enumerate(clist):
                K.mm(pc[:], C_[:, j, :], XE[:, slot, :], i == 0, i == len(clist) - 1, (C_, XE), (pc,))
            K.stt(Hcs[r][:], pc[:], wk[:, c:c + 1], rn[:, osl], ALU.mult, ALU.mult, (pc, wk, rn), (Hcs[r],))
            K.store("pool", Hcs[r], sc["Hc"][c][:, osl], Hcs[r][:])
            pc = K.bank(4 + r)
            for i, (j, slot) in enumerate(slist):
                K.mm(pc[:], S_[:, j, :], XO[:, slot, :], i == 0, i == len(slist) - 1, (S_, XO), (pc,))
            K.stt(Hss[r][:], pc[:], wk[:, c:c + 1], rn[:, osl], ALU.mult, ALU.mult, (pc, wk, rn), (Hss[r],))
            K.store("pool", Hss[r], sc["Hs"][c][:, osl], Hss[r][:])
            sconv_tile(o * 32 + c)


def phase_hyena(K, io, sc):
    K.reset()
    alt = K.alloc([128, 128], BF16)
    K.load("sp", alt, alt[:], io["alt"])
    Rb = K.alloc([128, 128], BF16)
    Eb_ = K.alloc([128, 128], BF16)
    R32 = K.alloc([128, 128], F32)
    E32 = K.alloc([128, 128], F32)
    K.load("sp", Rb, Rb[:], io["rev_rb"])
    K.load("sp", Eb_, Eb_[:], io["rev_eb"])
    K.load("sp", R32, R32[:], io["rev_r"])
    K.load("sp", E32, E32[:], io["rev_e"])
    sgn = K.alloc([128, 1], F32)
    K.load("sp", sgn, sgn[:], io["sgn"])
    X = K.alloc([128, 33, 512], BF16)
    Y = K.alloc([128, 32, 2, 512], BF16)
    Cs = K.ring(3, [128, 32, 128], BF16)
    Ss = K.ring(3, [128, 32, 128], BF16)
    Hts = K.ring(2, [128, 2, 256], F32)
    hn = K.alloc([128, 256], F32)
    YN = K.alloc([128, 512], F32)
    bias = K.alloc([128, 256], F32)
    t1s = K.ring(2, [128, 512], F32)
    t2s = K.ring(2, [128, 512], F32)
    x32s = K.ring(3, [128, 512], F32)
    gts = K.ring(3, [128, 512], F32)
    cvs = K.ring(3, [128, 512], F32)
    obs = K.ring(3, [128, 512], BF16)
    Ssbs = K.ring(2, [128, 512], F32)
    his = K.ring(2, [128, 512], F32)
    pX = [[K.bank(0), K.bank(1)], [K.bank(2), K.bank(3)]]
    pSA = [[K.bank(0), K.bank(1)], [K.bank(2), K.bank(3)]]
    pR = [K.bank(4), K.bank(5)]
    pN = K.bank(6)
    slc = 0
    ec = 0
    v3 = lambda ap: ap.rearrange("p (b c) -> p b c", b=2)
    sgr = K.ring(2, [128, 1024], F32)
    sgi = [0]
    NSG = 2 * (NT // 128)

    def gate_sigmoid_tile():
        hi_ = sgi[0]
        if hi_ >= NSG:
            return
        sgi[0] += 1
        ti, hc_ = divmod(hi_, 2)
        t = sgr[hi_ % 2]
        dst = sc["SG"][ti * 128:(ti + 1) * 128, hc_ * 1024:(hc_ + 1) * 1024]
        K.load("sp", t, t[:], dst)
        K.sigmoid(t[:], t[:], (t,), (t,), t)
        K.store("pool", t, dst, t[:])

    for o in range(2):
        src_bf = sc["hvb"] if o == 0 else sc["z1b"]
        src32 = sc["z"] if o == 0 else sc["z1"]
        for half in range(2):
            csl = slice(half * 256, (half + 1) * 256)
            hsl = slice(o * 512 + half * 256, o * 512 + (half + 1) * 256)
            gsl = slice(512 + o * 512 + half * 256, 512 + o * 512 + (half + 1) * 256)
            xv = X[:, 0:32, :].rearrange("p j (b c) -> p j b c", b=2)
            for b in range(NB):
                K.load("sp", X, xv[:, :, b, :], src_bf[b * L:(b + 1) * L, csl].rearrange("(j p) c -> p j c", p=128))
            K.load("sp", hn, hn[:], sc["HN"][:, hsl])
            K.load("sp", bias, bias[:], io["hy_bias"][:, hsl].partition_broadcast(128))
            for j in range(32):
                K.mm(pN[:], alt[:], X[:, j, :], j == 0, j == 31, (alt, X), (pN,))
            K.tt(v3(YN[:]), v3(pN[:]), hn[:].unsqueeze(1).to_broadcast([128, 2, 256]), ALU.mult, (pN, hn), (YN,))
            fold_inplace(K, X, 512, Rb, Eb_, [K.bank(4), K.bank(5)])
            tabs_f = {}

            def pre_f(c):
                nonlocal slc
                C_, S_ = Cs[slc % 3], Ss[slc % 3]
                slc += 1
                load_tab(K, C_, io["tfc"], c, 17)
                load_tab(K, S_, io["tfs"], c, 17)
                tabs_f[c] = (C_, S_)

            pre_f(0)
            for c in range(32):
                r = c % 2
                if c + 1 < 32:
                    pre_f(c + 1)
                C_, S_ = tabs_f.pop(c)
                Ht = Hts[r]
                K.load("sp", Ht, Ht[:, 0, :], sc["Hc"][c][:, hsl])
                K.load("sp", Ht, Ht[:, 1, :], sc["Hs"][c][:, hsl])
                pc, ps_ = pX[r]
                clist, slist = fwd_lists(c // 16)
                for i, (j, slot) in enumerate(clist):
                    K.mm(pc[:], C_[:, j, :], X[:, slot, :], i == 0, i == len(clist) - 1, (C_, X), (pc,))
                for i, (j, slot) in enumerate(slist):
                    K.mm(ps_[:], S_[:, j, :], X[:, slot, :], i == 0, i == len(slist) - 1, (S_, X), (ps_,))
                hc = Ht[:, 0, :].unsqueeze(1).to_broadcast([128, 2, 256])
                hs = Ht[:, 1, :].unsqueeze(1).to_broadcast([128, 2, 256])
                t1, t2 = t1s[r], t2s[r]
                K.tt(v3(t1[:]), v3(pc[:]), hc, ALU.mult, (pc, Ht), (t1,))
                K.tt(v3(t2[:]), v3(ps_[:]), hs, ALU.mult, (ps_, Ht), (t2,))
                K.tt(Y[:, c, 0, :], t1[:], t2[:], ALU.subtract, (t1, t2), (Y,))
                K.tt(v3(t1[:]), v3(pc[:]), hs, ALU.mult, (pc, Ht), (t1,))
                K.tt(v3(t2[:]), v3(ps_[:]), hc, ALU.mult, (ps_, Ht), (t2,))
                K.tt(Y[:, c, 1, :], t1[:], t2[:], ALU.add, (t1, t2), (Y,))
                gate_sigmoid_tile()

            def epilogue(ch, src_ap, src_tile):
                nonlocal ec
                x32, gt, cv, ob = x32s[ec % 3], gts[ec % 3], cvs[ec % 3], obs[ec % 3]
                ec += 1
                for b in range(NB):
                    t0 = b * L + ch * 128
                    K.load("sp", x32, x32[:, b * 256:(b + 1) * 256], src32[t0:t0 + 128, csl])
                    K.load("sp", gt, gt[:, b * 256:(b + 1) * 256], sc["z"][t0:t0 + 128, gsl])
                K.stt(cv[:], YN[:], sgn[:, 0:1], src_ap, ALU.mult, ALU.add, (YN, sgn, src_tile), (cv,))
                K.tt(v3(x32[:]), v3(x32[:]), bias[:].unsqueeze(1).to_broadcast([128, 2, 256]), ALU.mult, (x32, bias), (x32,), eng="pool")
                K.tt(cv[:], cv[:], x32[:], ALU.add, (cv, x32), (cv,))
                if o == 0:
                    K.tt(cv[:], cv[:], gt[:], ALU.mult, (cv, gt), (cv,))
                    K.act(ob[:], cv[:], AF.Copy, (cv,), (ob,))
                    for b in range(NB):
                        t0 = b * L + ch * 128
                        K.store("pool", cv, sc["z1"][t0:t0 + 128, csl], cv[:, b * 256:(b + 1) * 256])
                        K.store("pool", ob, sc["z1b"][t0:t0 + 128, csl], ob[:, b * 256:(b + 1) * 256])
                else:
                    K.tt(ob[:], cv[:], gt[:], ALU.mult, (cv, gt), (ob,))
                    for b in range(NB):
                        t0 = b * L + ch * 128
                        K.store("pool", ob, sc["ya"][t0:t0 + 128, csl], ob[:, b * 256:(b + 1) * 256])

            tabs_i = {}

            def pre_i(tc):
                nonlocal slc
                C_, S_ = Cs[slc % 3], Ss[slc % 3]
                slc += 1
                load_tab(K, C_, io["tic"], tc, 32)
                load_tab(K, S_, io["tis"], tc, 32)
                tabs_i[tc] = (C_, S_)

            pre_i(16)
            for it, tc in enumerate(range(16, -1, -1)):
                r = it % 2
                if tc - 1 >= 0:
                    pre_i(tc - 1)
                C_, S_ = tabs_i.pop(tc)
                pS_, pA_ = pSA[r]
                for c in range(32):
                    tab, cs_ = (C_, 0) if c < 16 else (S_, 1)
                    K.mm(pS_[:], tab[:, c, :], Y[:, c, cs_, :], c == 0, c == 31, (tab, Y), (pS_,))
                Ssb, hi = Ssbs[r], his[r]
                if tc == 16:
                    K.act(hi[:], pS_[:], AF.Copy, (pS_,), (hi,))
                    continue
                for i, c in enumerate(list(range(16, 32)) + list(range(0, 16))):
                    tab, cs_ = (C_, 0) if c >= 16 else (S_, 1)
                    K.mm(pA_[:], tab[:, c, :], Y[:, c, cs_, :], i == 0, i == 31, (tab, Y), (pA_,))
                K.act(Ssb[:], pS_[:], AF.Copy, (pS_,), (Ssb,))
                K.tt(hi[:], Ssb[:], pA_[:], ALU.subtract, (Ssb, pA_), (hi,))
                K.tt(Ssb[:], Ssb[:], pA_[:], ALU.add, (Ssb, pA_), (Ssb,))
                hi_prev = his[(it - 1) % 2]
                pr = pR[r]
                K.mm(pr[:], R32[:], hi[:], True, False, (R32, hi), (pr,))
                K.mm(pr[:], E32[:], hi_prev[:], False, True, (E32, hi_prev), (pr,))
                epilogue(tc, Ssb[:], Ssb)
                epilogue(31 - tc, pr[:], pr)
        K.S.barrier()
    while sgi[0] < NSG:
        gate_sigmoid_tile()


class IO:
    SHAPES = {
        "x": ([NT, D], F32), "p": ([NT, PLE], F32), "ident": ([128, 128], F32),
    }
    SHAPES.update({
        "mix_norm_pre": ([128, 8], F32), "w_in": ([D, INC], F32), "q_norm": ([1, 64], F32), "k_norm": ([1, 64], F32),
        "rope_cos": ([128, 1024], F32), "rope_sin": ([128, 1024], F32),
    })
    SHAPES.update({
        "mix_norm_post": ([1, D], F32), "w_hy_out": ([512, D], F32), "w_att_out": ([512, D], F32), "w_out": ([D, D], F32),
        "ple_norm_pre": ([128, 8], F32), "ple_norm_post": ([1, D], F32), "w_ple_gate": ([D, D], F32), "w_ple_proj": ([PLE, D], F32),
    })
    SHAPES.update({
        "filt_zT": ([33, L], F32), "filt_w1": ([33, 64], F32), "filt_w2": ([64, 64], F32), "filt_w3": ([64, 2048], F32),
        "filt_b1": ([64, 1], F32), "filt_freq1": ([64, 1], F32), "filt_b2": ([64, 1], F32), "filt_freq2": ([64, 1], F32),
        "filt_deltas": ([1, 2048], F32), "negt": ([128, 32], F32), "wk": ([128, 32], F32), "alt": ([128, 128], BF16),
        "sgn": ([128, 1], F32), "tfc": ([32, 128, 17 * 128], BF16), "tfs": ([32, 128, 17 * 128], BF16),
        "tic": ([17, 128, 4096], BF16), "tis": ([17, 128, 4096], BF16),
        "rev_r": ([128, 128], F32), "rev_e": ([128, 128], F32), "rev_rb": ([128, 128], BF16), "rev_eb": ([128, 128], BF16),
        "hy_bias": ([1, 1024], F32), "hy_short_w": ([1, 4608], F32), "hy_short_b": ([1, 1536], F32),
    })
    for _p in ("ffn1", "ffn2"):
        SHAPES[_p + "_norm_pre"] = ([128, 8], F32)
        SHAPES[_p + "_norm_post"] = ([1, D], F32)
        SHAPES[_p + "_w_gate"] = ([D, FF], F32)
        SHAPES[_p + "_w_up"] = ([D, FF], F32)
        SHAPES[_p + "_w_down"] = ([FF, D], F32)

    def __init__(self, nc):
        self.nc = nc
        self.d = {}

    def __getitem__(self, name):
        if name not in self.d:
            shape, dt = self.SHAPES[name]
            self.d[name] = self.nc.dram_tensor(name, list(shape), dt, kind="ExternalInput").ap()
        return self.d[name]


ALL_PHASES = ("ffn1", "win", "filt1", "filt2", "hyena", "attn", "merge", "ffn2", "ple")


def build_program(phases=ALL_PHASES, debug=()):
    nc = bass.Bass("TRN2", target_bir_lowering=False)
    io = IO(nc)
    y = nc.dram_tensor("y", [NT, D], F32, kind="ExternalOutput").ap()

    def scratch(name, shape, dt=F32):
        kind = "ExternalOutput" if name in debug else "Internal"
        return nc.dram_tensor(name, list(shape), dt, kind=kind).ap()

    sc = {}
    sc["x1"] = scratch("x1", [NT, D])
    sc["hy"] = scratch("hy", [NT, 1536])
    sc["SG"] = scratch("SG", [NT, 2048])
    sc["QT"] = scratch("QT", [NB, 8, 64, L], BF16)
    sc["KT"] = scratch("KT", [NB, 2, 64, L], BF16)
    sc["VA"] = scratch("VA", [NB, 2, L, 68], BF16)
    sc["ya"] = scratch("ya", [NT, 512], BF16)
    sc["yb"] = scratch("yb", [NT, 512], BF16)
    sc["x2"] = scratch("x2", [NT, D])
    sc["x3"] = scratch("x3", [NT, D])
    sc["Ebf"] = scratch("Ebf", [L, 1024], BF16)
    sc["Obf"] = scratch("Obf", [L, 1024], BF16)
    sc["rn"] = scratch("rn", [128, 1024])
    sc["HN"] = scratch("HN", [128, 1024])
    sc["Hc"] = scratch("Hc", [32, 128, 1024])
    sc["Hs"] = scratch("Hs", [32, 128, 1024])
    sc["z"] = scratch("z", [NT, 1536])
    sc["hvb"] = scratch("hvb", [NT, 512], BF16)
    sc["z1"] = scratch("z1", [NT, 512])
    sc["z1b"] = scratch("z1b", [NT, 512], BF16)
    x1 = sc["x1"]
    with contextlib.ExitStack() as st:
        arena = st.enter_context(nc.sbuf_tensor("arena", [128, ARENA_WORDS], F32))
        psum = st.enter_context(nc.psum_tensor("psum", [128, 4096], F32))
        banks = [psum[:, i * 512:(i + 1) * 512] for i in range(8)]
        K = KB(nc, arena, banks)
        K.psum = psum
        if "ffn1" in phases:
            phase_ffn(K, io, io["x"], y if phases[-1] == "ffn1" else x1, "ffn1")
        if "win" in phases:
            phase_win(K, io, sc)
        if "filt1" in phases:
            phase_filt1(K, io, sc)
        if "filt2" in phases:
            phase_filt2_sconv(K, io, sc)
        if "hyena" in phases:
            phase_hyena(K, io, sc)
        if "attn" in phases:
            phase_attn(K, io, sc)
        if "merge" in phases:
            phase_merge(K, io, sc)
        if "ffn2" in phases:
            phase_ffn(K, io, sc["x2"], sc["x3"], "ffn2", preloaded=("merge" in phases))
        if "ple" in phases:
            phase_ple(K, io, sc, sc["x3"], y)
        K.S.barrier()
        K.S.emit()
    return nc, sorted(io.d.keys())


_CONSTS = {}


def _hyena_consts():
    if _CONSTS:
        return _CONSTS
    f32 = np.float32
    N = 2 * L
    bf = ml_dtypes.bfloat16
    cidx = np.arange(32)
    kk = (2 * (128 * (cidx[:, None] % 16) + np.arange(128)[None, :]) + (cidx[:, None] // 16)).astype(np.int64)
    nn = (128 * np.arange(17)[:, None] + np.arange(128)[None, :]).astype(np.int64)
    m = (nn[None, :, :, None] * kk[:, None, None, :]) % N
    ang = (2.0 * np.pi / N) * m.astype(np.float64)
    _CONSTS["tfc"] = np.ascontiguousarray(np.cos(ang).astype(f32).transpose(0, 2, 1, 3).reshape(32, 128, 17 * 128).astype(bf))
    _CONSTS["tfs"] = np.ascontiguousarray(np.sin(ang).astype(f32).transpose(0, 2, 1, 3).reshape(32, 128, 17 * 128).astype(bf))
    tt_ = (128 * np.arange(17)[:, None] + np.arange(128)[None, :]).astype(np.int64)
    m = (kk[None, :, :, None] * tt_[:, None, None, :]) % N
    ang = (2.0 * np.pi / N) * m.astype(np.float64)
    _CONSTS["tic"] = np.ascontiguousarray(np.cos(ang).astype(f32).transpose(0, 2, 1, 3).reshape(17, 128, 32 * 128).astype(bf))
    _CONSTS["tis"] = np.ascontiguousarray(np.sin(ang).astype(f32).transpose(0, 2, 1, 3).reshape(17, 128, 32 * 128).astype(bf))
    del m, ang
    wk2 = np.where(kk == 0, 1.0 / N, 2.0 / N).astype(f32)
    _CONSTS["wk"] = np.ascontiguousarray(wk2.T)
    Rm = np.zeros((128, 128), f32)
    for q in range(1, 128):
        Rm[q, 128 - q] = 1.0
    Em = np.zeros((128, 128), f32)
    Em[0, 0] = 1.0
    _CONSTS["rev_r"] = Rm
    _CONSTS["rev_e"] = Em
    _CONSTS["rev_rb"] = np.ascontiguousarray(Rm.astype(bf))
    _CONSTS["rev_eb"] = np.ascontiguousarray(Em.astype(bf))
    t = np.linspace(0.0, 1.0, L, dtype=f32)
    w = ((2.0 * math.pi / L) * np.arange(L, dtype=f32)).astype(f32)
    bands = np.linspace(1e-4, 15.0, 16, dtype=f32)
    a2 = (w[:, None] * bands[None, :]).astype(f32)
    z = np.concatenate([t[:, None], np.cos(a2), -np.sin(a2)], axis=-1).astype(f32)
    _CONSTS["filt_zT"] = np.ascontiguousarray(z.T)
    _CONSTS["negt"] = np.ascontiguousarray((-t).reshape(32, 128).T)
    sg = np.where(np.arange(128) % 2 == 0, 1.0, -1.0).astype(f32)
    _CONSTS["alt"] = np.ascontiguousarray(np.repeat(sg[:, None], 128, axis=1).astype(ml_dtypes.bfloat16))
    _CONSTS["sgn"] = np.ascontiguousarray(sg.reshape(128, 1))
    return _CONSTS


def _col128(v):
    v = np.asarray(v, np.float32).reshape(-1, 128)
    return np.ascontiguousarray(v.T)


def make_in_maps(inputs):
    f32 = np.float32
    shared = {"ident": np.eye(128, dtype=f32)}
    for pfx in ("ffn1", "ffn2"):
        shared[pfx + "_norm_pre"] = _col128(inputs[pfx + "_norm_pre"][0])
        shared[pfx + "_norm_post"] = np.ascontiguousarray(inputs[pfx + "_norm_post"][0].reshape(1, D), f32)
        shared[pfx + "_w_gate"] = np.ascontiguousarray(inputs[pfx + "_w_gate"][0], f32)
        shared[pfx + "_w_up"] = np.ascontiguousarray(inputs[pfx + "_w_up"][0], f32)
        shared[pfx + "_w_down"] = np.ascontiguousarray(inputs[pfx + "_w_down"][0], f32)
    shared["mix_norm_post"] = np.ascontiguousarray(inputs["mix_norm_post"][0].reshape(1, D), f32)
    shared["ple_norm_post"] = np.ascontiguousarray(inputs["ple_norm_post"][0].reshape(1, D), f32)
    shared["ple_norm_pre"] = _col128(inputs["ple_norm_pre"][0])
    for nm in ("w_hy_out", "w_att_out", "w_out", "w_ple_gate", "w_ple_proj"):
        shared[nm] = np.ascontiguousarray(inputs[nm][0], f32)
    shared["mix_norm_pre"] = _col128(inputs["mix_norm_pre"][0])
    shared["w_in"] = np.ascontiguousarray(inputs["w_in"][0], f32)
    shared["q_norm"] = np.ascontiguousarray(inputs["q_norm"][0].reshape(1, 64), f32)
    shared["k_norm"] = np.ascontiguousarray(inputs["k_norm"][0].reshape(1, 64), f32)
    tpos = np.arange(L)
    inv = (10000.0 ** (-np.arange(0, 32, 2, dtype=np.float32) / 32)).astype(f32)
    ang = np.concatenate([(tpos // 64).astype(f32)[:, None] * inv, (tpos % 64).astype(f32)[:, None] * inv], axis=-1).astype(f32)
    shared["rope_cos"] = np.ascontiguousarray(np.cos(ang).astype(f32).reshape(32, 128, 32).transpose(1, 0, 2).reshape(128, 1024))
    shared["rope_sin"] = np.ascontiguousarray(np.sin(ang).astype(f32).reshape(32, 128, 32).transpose(1, 0, 2).reshape(128, 1024))
    shared.update(_hyena_consts())
    for nm in ("filt_w1", "filt_w2", "filt_w3"):
        shared[nm] = np.ascontiguousarray(inputs[nm][0], f32)
    for nm in ("filt_b1", "filt_freq1", "filt_b2", "filt_freq2"):
        shared[nm] = np.ascontiguousarray(inputs[nm][0].reshape(64, 1), f32)
    shared["filt_deltas"] = np.ascontiguousarray(inputs["filt_deltas"][0].reshape(1, 2048), f32)
    shared["hy_bias"] = np.ascontiguousarray(inputs["hy_bias"][0].reshape(1, 1024), f32)
    shared["hy_short_w"] = np.ascontiguousarray(inputs["hy_short_w"][0].reshape(1, 4608), f32)
    shared["hy_short_b"] = np.ascontiguousarray(inputs["hy_short_b"][0].reshape(1, 1536), f32)
    maps = []
    x = np.asarray(inputs["x"], f32)
    p = np.asarray(inputs["p"], f32)[0]
    for c in range(NCORES):
        m = dict(shared)
        m["x"] = np.ascontiguousarray(x[c * NB:(c + 1) * NB].reshape(NT, D))
        m["p"] = np.ascontiguousarray(p[c * NB:(c + 1) * NB].reshape(NT, PLE))
        maps.append(m)
    return maps


def kernel(**inputs):
    nc, used = build_program(phases=ALL_PHASES)
    maps = [{k: m[k] for k in used} for m in make_in_maps(inputs)]
    res = run_bass_kernel_spmd(nc, maps, core_ids=list(range(NCORES)))
    out = np.concatenate([np.asarray(r["y"], np.float32).reshape(NB, L, D) for r in res.results], axis=0)
    return out
```

```python
import math
import contextlib
import numpy as np
import ml_dtypes
import concourse.bass as bass
import concourse.mybir as mybir
from concourse.bass_utils import run_bass_kernel_spmd

F32 = mybir.dt.float32
BF16 = mybir.dt.bfloat16
AF = mybir.ActivationFunctionType
ALU = mybir.AluOpType
AX = mybir.AxisListType

NCORES = 8
L = 4096
NB = 2
NT = NB * L
D = 1024
FF = 2816
HYW = 512
INC = 4352
PLE = 256
EPS = 1e-6
SEM_ROLL = 30000
ARENA_WORDS = 52480


class Res:
    __slots__ = ("last_w", "reads")

    def __init__(self):
        self.last_w = None
        self.reads = []


class Tile:
    def __init__(self, ap, slot):
        self.ap = ap
        self.r = Res()
        self.slot = slot

    def __getitem__(self, k):
        return self.ap[k]


class Sched:
    ENGS = ("pe", "act", "dve", "pool", "sp")

    def __init__(self, nc):
        self.nc = nc
        self.ops = {e: [] for e in self.ENGS}
        self.cnt = {e: 0 for e in self.ENGS}
        self.epoch = {e: 0 for e in self.ENGS}
        self.known = {e: {} for e in self.ENGS}
        self.semkeys = []
        self.dma_slots = {}
        self.total = {}
        for e in self.ENGS:
            key = ("eng", e, 0)
            self.semkeys.append(key)
            self.total[key] = 0

    def _eng_token(self, e):
        if self.cnt[e] >= SEM_ROLL:
            self.epoch[e] += 1
            self.cnt[e] = 0
            self.semkeys.append(("eng", e, self.epoch[e]))
        self.cnt[e] += 1
        key = ("eng", e, self.epoch[e])
        self.total[key] = self.cnt[e]
        return (key, self.cnt[e], e)

    def _dma_token(self, slot):
        st = self.dma_slots.get(slot)
        if st is None or st[1] + 16 > SEM_ROLL:
            ep = 0 if st is None else st[2] + 1
            key = ("dma", slot, ep)
            self.semkeys.append(key)
            st = [key, 0, ep]
            self.dma_slots[slot] = st
        st[1] += 16
        self.total[st[0]] = st[1]
        return (st[0], st[1], "dma")

    def _need(self, e, tok, skip_same):
        if tok is None:
            return
        key, val, src = tok
        if src == e and (skip_same or e == "pe"):
            return
        if key[0] == "dma":
            val = max(val, self.total.get(key, 0))
        k = self.known[e]
        if k.get(key, 0) >= val:
            return
        k[key] = val
        self.ops[e].append(("wait", key, val))

    def _deps(self, e, reads, writes):
        for t in reads:
            self._need(e, t.r.last_w, False)
        for t in writes:
            self._need(e, t.r.last_w, False)
            for tok in t.r.reads:
                self._need(e, tok, False)

    def _commit(self, tok, reads, writes):
        for t in reads:
            rr = t.r.reads
            rr.append(tok)
            if len(rr) > 48:
                best = {}
                for x in rr:
                    if x[0] not in best or best[x[0]][1] < x[1]:
                        best[x[0]] = x
                t.r.reads = list(best.values())
        for t in writes:
            t.r.last_w = tok
            t.r.reads = []

    def op(self, e, fn, rd=(), wr=()):
        self._deps(e, rd, wr)
        tok = self._eng_token(e)
        self.ops[e].append(("op", fn, tok[0], 1))
        self._commit(tok, rd, wr)

    def dma(self, e, slot, fn, rd=(), wr=()):
        self._deps(e, rd, wr)
        tok = self._dma_token(slot)
        self.ops[e].append(("op", fn, tok[0], 16))
        self._commit(tok, rd, wr)

    def barrier(self):
        for e in self.ENGS:
            for key in list(self.semkeys):
                v = self.total.get(key, 0)
                if v <= 0:
                    continue
                k = self.known[e]
                if k.get(key, 0) >= v:
                    continue
                k[key] = v
                self.ops[e].append(("wait", key, v))

    def emit(self):
        nc = self.nc
        with contextlib.ExitStack() as st:
            sems = {}
            assert len(self.semkeys) <= 96, ("too many semaphores", len(self.semkeys))
            for i, key in enumerate(self.semkeys):
                sems[key] = st.enter_context(nc.semaphore("s%d" % i))
            block = st.enter_context(nc.Block())

            def run(eng, lst):
                for it in lst:
                    if it[0] == "wait":
                        eng.wait_ge(sems[it[1]], it[2])
                    else:
                        it[1](eng).then_inc(sems[it[2]], it[3])

            ops = self.ops

            @block.tensor
            def _(eng):
                run(eng, ops["pe"])

            @block.scalar
            def _(eng):
                run(eng, ops["act"])

            @block.vector
            def _(eng):
                run(eng, ops["dve"])

            @block.gpsimd
            def _(eng):
                run(eng, ops["pool"])

            @block.sync
            def _(eng):
                run(eng, ops["sp"])


class KB:
    def __init__(self, nc, arena, banks):
        self.nc = nc
        self.S = Sched(nc)
        self.arena = arena
        self.banks = banks
        self.off = 0
        self.nslot = 0
        self.bank_tiles = [Tile(b, None) for b in banks]

    def reset(self):
        self.S.barrier()
        self.off = 0
        self.nslot = 0
        for t in self.bank_tiles:
            t.r = Res()
        self.cst = self.alloc([128, 2], F32)
        self.memset(self.cst[:, 0:1], EPS, (self.cst,))
        self.memset(self.cst[:, 1:2], 1.0, (self.cst,))

    def alloc(self, shape, dt, parts=128):
        n = 1
        for s in shape[1:]:
            n *= s
        words = (n * (2 if dt == BF16 else 4) + 3) // 4
        words = (words + 7) // 8 * 8
        assert self.off + words <= ARENA_WORDS, ("arena overflow", self.off, words)
        ap = self.arena[0:shape[0], self.off:self.off + words]
        self.off += words
        if dt == BF16:
            ap = ap.bitcast(BF16)
        ap = ap[:, 0:n]
        if len(shape) == 3:
            ap = ap.rearrange("p (a b) -> p a b", a=shape[1])
        elif len(shape) == 4:
            ap = ap.rearrange("p (a b c) -> p a b c", a=shape[1], b=shape[2])
        t = Tile(ap, "d%d" % self.nslot)
        self.nslot += 1
        return t

    def ring(self, n, shape, dt):
        return [self.alloc(shape, dt) for _ in range(n)]

    def bank(self, i):
        return self.bank_tiles[i]

    def dma(self, q, tile, out, in_, rd=(), wr=()):
        self.S.dma(q, tile.slot + ("p" if q == "pool" else "s"), lambda e: e.dma_start(out=out, in_=in_), rd, wr)

    def load(self, q, tile, out, in_):
        self.dma(q, tile, out, in_, (), (tile,))

    def store(self, q, tile, out, in_):
        self.dma(q, tile, out, in_, (tile,), ())

    def mm(self, out, lhsT, rhs, start, stop, rd, wr):
        self.S.op("pe", lambda e: e.matmul(out, lhsT=lhsT, rhs=rhs, start=start, stop=stop), rd, wr)

    def tr(self, out, in_, ident, rd, wr):
        self.S.op("pe", lambda e: e.transpose(out=out, in_=in_, identity=ident), rd, wr)

    def act(self, out, in_, func, rd, wr, **kw):
        self.S.op("act", lambda e: e.activation(out=out, in_=in_, func=func, **kw), rd, wr)

    def tt(self, out, a, b, op, rd, wr, eng="dve"):
        self.S.op(eng, lambda e: e.tensor_tensor(out=out, in0=a, in1=b, op=op), rd, wr)

    def ts(self, out, a, s1, s2, op0, op1, rd, wr, eng="dve"):
        if s2 is None:
            self.S.op(eng, lambda e: e.tensor_scalar(out=out, in0=a, scalar1=s1, scalar2=None, op0=op0), rd, wr)
        else:
            self.S.op(eng, lambda e: e.tensor_scalar(out=out, in0=a, scalar1=s1, scalar2=s2, op0=op0, op1=op1), rd, wr)

    def stt(self, out, a, s, b, op0, op1, rd, wr, eng="dve"):
        self.S.op(eng, lambda e: e.scalar_tensor_tensor(out=out, in0=a, scalar=s, in1=b, op0=op0, op1=op1), rd, wr)

    def cp(self, out, in_, rd, wr, eng="dve"):
        self.S.op(eng, lambda e: e.tensor_copy(out=out, in_=in_), rd, wr)

    def recip(self, out, in_, rd, wr):
        self.S.op("dve", lambda e: e.reciprocal(out=out, in_=in_), rd, wr)

    def memset(self, ap, val, wr, eng="dve"):
        self.S.op(eng, lambda e: e.memset(ap, val), (), wr)

    def reduce(self, out, in_, rd, wr):
        self.S.op("dve", lambda e: e.tensor_reduce(out=out, in_=in_, axis=AX.X, op=ALU.add), rd, wr)

    def rstd_from_ssq(self, st, n):
        self.act(st[:, 1:2], st[:, 0:1], AF.Ln, (st, self.cst), (st,), scale=1.0 / n, bias=self.cst[:, 0:1])
        self.act(st[:, 1:2], st[:, 1:2], AF.Exp, (st,), (st,), scale=-0.5)

    def sigmoid(self, out, in_, rd, wr, tmp):
        self.act(tmp[:], in_, AF.Exp, rd, (tmp,), scale=-1.0)
        self.act(tmp[:], tmp[:], AF.Ln, (tmp, self.cst), (tmp,), bias=self.cst[:, 1:2])
        self.act(out, tmp[:], AF.Exp, (tmp,), wr, scale=-1.0)


def load_ident(K, io):
    idf = K.alloc([128, 128], F32)
    idb = K.alloc([128, 128], BF16)
    K.load("sp", idf, idf[:], io["ident"])
    K.cp(idb[:], idf[:], (idf,), (idb,))
    return idb


def norm_a(K, xt, st, xn, junk, eng="act"):
    K.act(junk[:], xt[:], AF.Square, (xt,), (junk, st), accum_out=st[:, 0:1])
    K.rstd_from_ssq(st, D)
    if eng == "act":
        K.act(xn[:], xt[:], AF.Copy, (xt, st), (xn,), scale=st[:, 1:2])
    else:
        K.ts(xn[:], xt[:], st[:, 1:2], None, ALU.mult, None, (xt, st), (xn,))


def norm_b(K, xn, idb, xnT_dst, dst_tile, pT, eng="dve"):
    pTv = pT.ap.bitcast(BF16).rearrange("p (a b) -> p a b", a=8)
    for kc in range(8):
        K.tr(pTv[:, kc, :], xn[:, kc * 128:(kc + 1) * 128], idb[:], (xn, idb), (pT,))
    if eng == "dve":
        K.cp(xnT_dst, pTv, (pT,), (dst_tile,))
    else:
        K.act(xnT_dst, pTv, AF.Copy, (pT,), (dst_tile,))


def norm_transpose(K, xt, idb, xnT_dst, dst_tile, st, xn, junk, pT, gbc=None):
    norm_a(K, xt, st, xn, junk)
    norm_b(K, xn, idb, xnT_dst, dst_tile, pT)


def post_norm_residual(K, po, xt, gbc, st, junk, outt, scale_rows=None):
    K.act(junk[:, 0:512], po[0][:], AF.Square, (po[0],), (junk, st), accum_out=st[:, 2:3])
    K.act(junk[:, 512:1024], po[1][:], AF.Square, (po[1],), (junk, st), accum_out=st[:, 3:4])
    K.tt(st[:, 0:1], st[:, 2:3], st[:, 3:4], ALU.add, (st,), (st,))
    K.rstd_from_ssq(st, D)
    for dh in range(2):
        sl = slice(dh * 512, (dh + 1) * 512)
        K.stt(outt[:, sl], po[dh][:], st[:, 1:2], gbc[:, sl], ALU.mult, ALU.mult, (po[dh], st, gbc), (outt,))
    K.tt(outt[:], outt[:], xt[:], ALU.add, (outt, xt), (outt,))


def load_weight_bf16(K, wt, w_dram, nk, gcol=None):
    for kc in range(nk):
        K.load("pool", wt, wt[:, kc, :], w_dram[kc * 128:(kc + 1) * 128, :])
    if gcol is not None:
        for kc in range(nk):
            K.ts(wt[:, kc, :], wt[:, kc, :], gcol[:, kc:kc + 1], None, ALU.mult, None, (wt, gcol), (wt,))


def ffn_weights(K, io, pfx, load=("g", "u", "d"), alloc_d=True):
    gpre = K.alloc([128, 8], F32)
    Wg = K.alloc([128, 8, FF], BF16)
    Wu = K.alloc([128, 8, FF], BF16)
    Wd = K.alloc([128, 22, D], BF16) if alloc_d else None
    if "g" in load:
        K.load("sp", gpre, gpre[:], io[pfx + "_norm_pre"])
        load_weight_bf16(K, Wg, io[pfx + "_w_gate"], 8, gpre)
    if "u" in load:
        load_weight_bf16(K, Wu, io[pfx + "_w_up"], 8, gpre)
    if "d" in load:
        load_weight_bf16(K, Wd, io[pfx + "_w_down"], 22)
    return Wg, Wu, Wd


def phase_ffn(K, io, xin, xout, pfx, preloaded=False):
    K.reset()
    Wg, Wu, Wd = ffn_weights(K, io, pfx, load=(("d",) if preloaded else ("g", "u", "d")))
    idb = load_ident(K, io)
    gbc = K.alloc([128, D], F32)
    K.load("sp", gbc, gbc[:], io[pfx + "_norm_post"].partition_broadcast(128))
    K.ts(gbc[:], gbc[:], 0.5, None, ALU.mult, None, (gbc,), (gbc,))
    xas = K.ring(2, [128, D], F32)
    xbs = K.ring(2, [128, D], F32)
    xnT = K.alloc([128, 8, 512], BF16)
    hT = K.alloc([128, 22, 512], BF16)
    xns = K.ring(2, [128, D], BF16)
    junk = K.alloc([128, D], BF16)
    sls = K.ring(2, [128, 512], F32)
    outs = K.ring(2, [128, D], F32)
    sts = K.ring(4, [128, 4], F32)
    st2 = K.ring(2, [128, 4], F32)
    pT = K.bank(0)
    pG = [K.bank(1), K.bank(2)]
    pU = [K.bank(3), K.bank(4)]
    pOs = [[K.bank(5), K.bank(6)], [K.bank(7), K.bank(4)]]
    nblk = NT // 512
    cnt = {"p": 0}

    def prep(blk, tt):
        t0 = blk * 512 + tt * 128
        i = cnt["p"]
        cnt["p"] += 1
        xa = xas[i % 2]
        K.load("sp", xa, xa[:], xin[t0:t0 + 128, :])
        norm_transpose(K, xa, idb, xnT[:, :, tt * 128:(tt + 1) * 128], xnT, sts[i % 4], xns[i % 2], junk, pT)

    for tt in range(4):
        prep(0, tt)
    for blk in range(nblk):
        for f in range(22):
            g = pG[f % 2]
            u = pU[f % 2]
            for kc in range(8):
                K.mm(g[:], Wg[:, kc, f * 128:(f + 1) * 128], xnT[:, kc, :], kc == 0, kc == 7, (Wg, xnT), (g,))
            for kc in range(8):
                K.mm(u[:], Wu[:, kc, f * 128:(f + 1) * 128], xnT[:, kc, :], kc == 0, kc == 7, (Wu, xnT), (u,))
            sl = sls[f % 2]
            K.sigmoid(sl[:], g[:], (g,), (sl,), sl)
            K.tt(sl[:], sl[:], g[:], ALU.mult, (sl, g), (sl,))
            K.tt(hT[:, f, :], sl[:], u[:], ALU.mult, (sl, u), (hT,))
        for tt in range(4):
            t0 = blk * 512 + tt * 128
            pO = pOs[tt % 2]
            xb = xbs[tt % 2]
            K.load("sp", xb, xb[:], xin[t0:t0 + 128, :])
            for dh in range(2):
                for f in range(22):
                    K.mm(pO[dh][:], hT[:, f, tt * 128:(tt + 1) * 128], Wd[:, f, dh * 512:(dh + 1) * 512],
                         f == 0, f == 21, (hT, Wd), (pO[dh],))
            if blk + 1 < nblk:
                prep(blk + 1, tt)
            outt = outs[tt % 2]
            post_norm_residual(K, pO, xb, gbc, st2[tt % 2], junk, outt)
            K.store("pool", outt, xout[t0:t0 + 128, :], outt[:])


def headnorm_rope(K, src_ps, src_tile, nh, g_bc, cosj, sinj, tabs, w, out_bf, out_tile):
    sq, st8, qn, t0, t1 = w
    n = nh * 64
    K.act(sq[:, 0:n], src_ps, AF.Square, (src_tile,), (sq,))
    K.reduce(st8[:, 0:nh], sq[:, 0:n].rearrange("p (h d) -> p h d", h=nh), (sq,), (st8,))
    K.act(st8[:, 8:8 + nh], st8[:, 0:nh], AF.Ln, (st8, K.cst), (st8,), scale=1.0 / 64, bias=K.cst[:, 0:1])
    K.act(st8[:, 8:8 + nh], st8[:, 8:8 + nh], AF.Exp, (st8,), (st8,), scale=-0.5)
    qv = qn[:, 0:n].rearrange("p (h d) -> p h d", h=nh)
    K.tt(qv, src_ps.rearrange("p (h d) -> p h d", h=nh), st8[:, 8:8 + nh].unsqueeze(2).to_broadcast([128, nh, 64]),
         ALU.mult, (src_tile, st8), (qn,))
    K.tt(qv, qv, g_bc[:, :].unsqueeze(1).to_broadcast([128, nh, 64]), ALU.mult, (qn, g_bc), (qn,))
    x0 = qv[:, :, 0:64:2]
    x1 = qv[:, :, 1:64:2]
    cb = cosj.unsqueeze(1).to_broadcast([128, nh, 32])
    sb = sinj.unsqueeze(1).to_broadcast([128, nh, 32])
    a = t0[:, 0:nh * 32].rearrange("p (h d) -> p h d", h=nh)
    b = t1[:, 0:nh * 32].rearrange("p (h d) -> p h d", h=nh)
    K.tt(a, x0, cb, ALU.mult, (qn, tabs), (t0,))
    K.tt(b, x1, sb, ALU.mult, (qn, tabs), (t1,))
    K.tt(out_bf[:, :, 0:64:2], a, b, ALU.subtract, (t0, t1), (out_tile,))
    K.tt(a, x0, sb, ALU.mult, (qn, tabs), (t0,))
    K.tt(b, x1, cb, ALU.mult, (qn, tabs), (t1,))
    K.tt(out_bf[:, :, 1:64:2], a, b, ALU.add, (t0, t1), (out_tile,))


def phase_win(K, io, sc):
    K.reset()
    idb = load_ident(K, io)
    gpre = K.alloc([128, 8], F32)
    K.load("sp", gpre, gpre[:], io["mix_norm_pre"])
    Win = K.alloc([128, 8, INC], BF16)
    load_weight_bf16(K, Win, io["w_in"], 8, gpre)
    gq = K.alloc([128, 64], F32)
    gk = K.alloc([128, 64], F32)
    K.load("sp", gq, gq[:], io["q_norm"].partition_broadcast(128))
    K.load("sp", gk, gk[:], io["k_norm"].partition_broadcast(128))
    tabs = K.alloc([128, 2, 32, 32], F32)
    K.load("sp", tabs, tabs[:, 0, :, :], io["rope_cos"].rearrange("p (j i) -> p j i", j=32))
    K.load("sp", tabs, tabs[:, 1, :, :], io["rope_sin"].rearrange("p (j i) -> p j i", j=32))
    xts = K.ring(2, [128, D], F32)
    xns = K.ring(2, [128, D], BF16)
    junk = K.alloc([128, D], BF16)
    uTs = K.ring(2, [128, 8, 128], BF16)
    sts = K.ring(2, [128, 4], F32)
    hyts = K.ring(2, [128, 1536], F32)
    sgts = K.ring(2, [128, 2048], F32)
    sgtmp = K.ring(2, [128, 512], F32)
    w = (K.alloc([128, 512], F32), K.alloc([128, 16], F32), K.alloc([128, 512], F32),
         K.alloc([128, 256], F32), K.alloc([128, 256], F32))
    qrs = K.ring(2, [128, 8, 64], BF16)
    krs = K.ring(2, [128, 2, 64], BF16)
    qTs = K.ring(2, [64, 8, 128], BF16)
    kTs = K.ring(2, [64, 2, 128], BF16)
    vas = K.ring(2, [128, 2, 68], BF16)
    for va in vas:
        K.memset(va[:], 0.0, (va,))
        K.memset(va[:, :, 64:65], 1.0, (va,))
    pT = K.bank(0)
    pQ = K.bank(7)
    ring = [K.bank(i) for i in range(1, 7)]
    rc = 0
    chunks = [(0, 512), (512, 512), (1024, 512), (1536, 512), (2048, 256),
              (2304, 512), (2816, 512), (3328, 512), (3840, 512)]
    def prep_a(ti):
        xt = xts[ti % 2]
        K.load("sp", xt, xt[:], sc["x1"][ti * 128:(ti + 1) * 128, :])
        norm_a(K, xt, sts[ti % 2], xns[ti % 2], junk)

    def prep_b(ti):
        norm_b(K, xns[ti % 2], idb, uTs[ti % 2][:], uTs[ti % 2], pT)

    prep_a(0)
    prep_b(0)
    order = [3, 4, 0, 1, 2, 5, 6, 7, 8]
    for ti in range(NT // 128):
        b, j = divmod(ti, 32)
        t0 = ti * 128
        uT = uTs[ti % 2]
        if ti + 1 < NT // 128:
            prep_a(ti + 1)
        hyt = hyts[ti % 2]
        sgt = sgts[ti % 2]
        qr = qrs[ti % 2]
        kr = krs[ti % 2]
        for ci in order:
            c0, cw = chunks[ci]
            pb = ring[rc % 6]
            rc += 1
            for kc in range(8):
                K.mm(pb[:, 0:cw], uT[:, kc, :], Win[:, kc, c0:c0 + cw], kc == 0, kc == 7, (uT, Win), (pb,))
            if ci < 3:
                K.cp(hyt[:, c0:c0 + 512], pb[:], (pb,), (hyt,))
            elif ci == 3:
                headnorm_rope(K, pb[:], pb, 8, gq, tabs[:, 0, j, :], tabs[:, 1, j, :], tabs, w, qr[:], qr)
            elif ci == 4:
                headnorm_rope(K, pb[:, 0:128], pb, 2, gk, tabs[:, 0, j, :], tabs[:, 1, j, :], tabs, w, kr[:], kr)
                va = vas[ti % 2]
                K.cp(va[:, :, 0:64], pb[:, 128:256].rearrange("p (g d) -> p g d", g=2), (pb,), (va,))
                K.store("pool", va, sc["VA"][b].rearrange("g t c -> t g c")[j * 128:(j + 1) * 128, :, :], va[:])
            else:
                K.act(sgt[:, c0 - 2304:c0 - 2304 + 512], pb[:], AF.Copy, (pb,), (sgt,))
        if ti + 1 < NT // 128:
            prep_b(ti + 1)
        pQv = pQ.ap.bitcast(BF16)
        for h in range(8):
            K.tr(pQv[0:64, h * 128:(h + 1) * 128], qr[:, h, :], idb[:], (qr, idb), (pQ,))
        qT = qTs[ti % 2]
        K.cp(qT[:], pQv[0:64, :].rearrange("p (h t) -> p h t", h=8), (pQ,), (qT,))
        K.store("pool", qT, sc["QT"][b].rearrange("h d t -> d h t")[:, :, j * 128:(j + 1) * 128], qT[:])
        pKb = ring[rc % 6]
        rc += 1
        pKv = pKb.ap.bitcast(BF16)
        for g in range(2):
            K.tr(pKv[0:64, g * 128:(g + 1) * 128], kr[:, g, :], idb[:], (kr, idb), (pKb,))
        kT = kTs[ti % 2]
        K.cp(kT[:], pKv[0:64, 0:256].rearrange("p (g t) -> p g t", g=2), (pKb,), (kT,))
        K.store("pool", kT, sc["KT"][b].rearrange("g d t -> d g t")[:, :, j * 128:(j + 1) * 128], kT[:])
        K.store("pool", hyt, sc["hy"][t0:t0 + 128, :], hyt[:])
        K.store("pool", sgt, sc["SG"][t0:t0 + 128, :], sgt[:])


def phase_attn(K, io, sc):
    K.reset()
    idf = K.alloc([128, 128], F32)
    K.load("sp", idf, idf[:], io["ident"])
    gq = K.alloc([128, 64], F32)
    gk = K.alloc([128, 64], F32)
    K.load("sp", gq, gq[:], io["q_norm"].partition_broadcast(128))
    K.load("sp", gk, gk[:], io["k_norm"].partition_broadcast(128))
    mx = K.alloc([128, 4], F32)
    K.S.op("dve", lambda e: e.tensor_reduce(out=mx[:, 0:1], in_=gq[:], axis=AX.X, op=ALU.max, apply_absolute_value=True), (gq,), (mx,))
    K.S.op("dve", lambda e: e.tensor_reduce(out=mx[:, 1:2], in_=gk[:], axis=AX.X, op=ALU.max, apply_absolute_value=True), (gk,), (mx,))
    K.tt(mx[:, 2:3], mx[:, 0:1], mx[:, 1:2], ALU.mult, (mx,), (mx,))
    K.ts(mx[:, 3:4], mx[:, 2:3], -8.0, None, ALU.mult, None, (mx,), (mx,))
    KTs = K.ring(2, [128, 2, L], BF16)
    VAs = K.ring(2, [128, 2, 32, 68], BF16)
    QTs = K.ring(2, [128, 8, L], BF16)
    PTs = K.ring(5, [128, 1024], BF16)
    oTs = K.ring(2, [68, 512], F32)
    ybts = K.ring(2, [128, 4, 512], BF16)
    rdens = K.ring(2, [128, 4], F32)
    pOT = [K.bank(4), K.bank(5)]
    pTr = [K.bank(6), K.bank(7)]
    for b in range(NB):
        K.memset(KTs[b][64:128, :, :], 0.0, (KTs[b],))
        K.memset(QTs[b][64:128, :, :], 0.0, (QTs[b],))
        for g in range(2):
            K.load("sp", KTs[b], KTs[b][0:64, g, :], sc["KT"][b, g])
            K.load("sp", VAs[b], VAs[b][:, g, :, :], sc["VA"][b, g].rearrange("(kc p) c -> p kc c", p=128))
        for h in range(8):
            K.load("sp", QTs[b], QTs[b][0:64, h, :], sc["QT"][b, h])
    items = [(b, qb, h) for b in range(NB) for qb in range(L // 512) for h in range(8)]
    state = {"sc": 0, "fin": []}

    def s_pair(it, kp):
        b, qb, h = it
        g = h // 4
        i = state["sc"]
        state["sc"] += 1
        b0 = 2 * (i % 2)
        pt = PTs[i % 5]
        for u in range(2):
            kc = 2 * kp + u
            ps = K.bank(b0 + u)
            K.mm(ps[:], KTs[b][:, g, kc * 128:(kc + 1) * 128], QTs[b][:, h, qb * 512:(qb + 1) * 512], True, True, (KTs[b], QTs[b]), (ps,))
        K.act(pt[:], K.psum[:, b0 * 512:(b0 + 2) * 512], AF.Exp, (K.bank(b0), K.bank(b0 + 1), mx), (pt,), scale=0.125, bias=mx[:, 3:4])
        return pt

    def pv_pair(pi, kp, pt):
        pb, pqb, ph = items[pi]
        for u in range(2):
            kc = 2 * kp + u
            K.mm(pOT[pi % 2][0:68, :], VAs[pb][:, ph // 4, kc, :], pt[:, u * 512:(u + 1) * 512], kc == 0, kc == 31, (VAs[pb], pt), (pOT[pi % 2],))
        if kp == 15:
            state["fin"].append(pi)

    def finish(idx):
        b, qb, h = items[idx]
        po, oT, ptr, rden, ybt = pOT[idx % 2], oTs[idx % 2], pTr[idx % 2], rdens[idx % 2], ybts[(idx // 8) % 2]
        K.cp(oT[:], po[0:68, :], (po,), (oT,))
        for qt in range(4):
            K.tr(ptr[:, qt * 68:(qt + 1) * 68], oT[:, qt * 128:(qt + 1) * 128], idf[0:68, 0:68], (oT, idf), (ptr,))
        pv = ptr[:, 0:272].rearrange("p (q c) -> p q c", q=4)
        K.recip(rden[:].unsqueeze(2), pv[:, :, 64:65], (ptr,), (rden,))
        K.tt(ybt[:, :, h * 64:(h + 1) * 64], pv[:, :, 0:64], rden[:].unsqueeze(2).to_broadcast([128, 4, 64]), ALU.mult, (ptr, rden), (ybt,))
        if h == 7:
            t0 = b * L + qb * 512
            K.store("pool", ybt, sc["yb"][t0:t0 + 512, :].rearrange("(q p) c -> p q c", p=128), ybt[:])

    pend = []
    for idx, it in enumerate(items):
        for kp in range(16):
            pt = s_pair(it, kp)
            pend.append((idx, kp, pt))
            if len(pend) > 2:
                pv_pair(*pend.pop(0))
            if kp == 4 and state["fin"]:
                finish(state["fin"].pop(0))
    while pend:
        pv_pair(*pend.pop(0))
    while state["fin"]:
        finish(state["fin"].pop(0))


def transpose_bf(K, src, src_tile, n, idb, pT, dst, dst_tile, eng="dve"):
    pTv = pT.ap.bitcast(BF16)[:, 0:n * 128].rearrange("p (a b) -> p a b", a=n)
    for kc in range(n):
        K.tr(pTv[:, kc, :], src[:, kc * 128:(kc + 1) * 128], idb[:], (src_tile, idb), (pT,))
    if eng == "dve":
        K.cp(dst, pTv, (pT,), (dst_tile,))
    else:
        K.act(dst, pTv, AF.Copy, (pT,), (dst_tile,))


def phase_merge(K, io, sc, prefetch_ffn2=True):
    K.reset()
    if prefetch_ffn2:
        ffn_weights(K, io, "ffn2", load=("g", "u"), alloc_d=False)
    idb = load_ident(K, io)
    gbc = K.alloc([128, D], F32)
    K.load("sp", gbc, gbc[:], io["mix_norm_post"].partition_broadcast(128))
    Wa = K.alloc([128, 4, D], BF16)
    Wb = K.alloc([128, 4, D], BF16)
    Wo = K.alloc([128, 8, D], BF16)
    load_weight_bf16(K, Wa, io["w_hy_out"], 4)
    load_weight_bf16(K, Wb, io["w_att_out"], 4)
    load_weight_bf16(K, Wo, io["w_out"], 8)
    xts = K.ring(3, [128, D], F32)
    yas = K.ring(2, [128, 512], BF16)
    ybs = K.ring(2, [128, 512], BF16)
    sgs = K.ring(2, [128, 2048], F32)
    yaTs = K.ring(2, [128, 4, 128], BF16)
    ybTs = K.ring(2, [128, 4, 128], BF16)
    m1s = K.ring(2, [128, 512], F32)
    m2s = K.ring(2, [128, 512], F32)
    mts = K.ring(3, [128, D], BF16)
    mTs = K.ring(3, [128, 8, 128], BF16)
    outs = K.ring(3, [128, D], F32)
    sts = K.ring(2, [128, 4], F32)
    junk = K.alloc([128, D], BF16)
    pT = [K.bank(0), K.bank(7)]
    pA = [K.bank(1), K.bank(2)]
    pB = [K.bank(3), K.bank(4)]
    pO = [K.bank(5), K.bank(6)]
    ntile = NT // 128

    def prep(ti):
        t0 = ti * 128
        r = ti % 2
        ya, yb, sg = yas[r], ybs[r], sgs[r]
        K.load("sp", ya, ya[:], sc["ya"][t0:t0 + 128, :])
        K.load("sp", yb, yb[:], sc["yb"][t0:t0 + 128, :])
        K.load("sp", sg, sg[:], sc["SG"][t0:t0 + 128, :])
        transpose_bf(K, ya, ya, 4, idb, pT[0], yaTs[r][:], yaTs[r], eng="act")
        transpose_bf(K, yb, yb, 4, idb, pT[1], ybTs[r][:], ybTs[r], eng="act")
        K.act(sg[:], sg[:], AF.Sigmoid, (sg,), (sg,))

    def stage1(ti):
        r = ti % 2
        sg, mt = sgs[r], mts[ti % 3]
        for dh in range(2):
            sl = slice(dh * 512, (dh + 1) * 512)
            for kc in range(4):
                K.mm(pA[dh][:], yaTs[r][:, kc, :], Wa[:, kc, sl], kc == 0, kc == 3, (yaTs[r], Wa), (pA[dh],))
            for kc in range(4):
                K.mm(pB[dh][:], ybTs[r][:, kc, :], Wb[:, kc, sl], kc == 0, kc == 3, (ybTs[r], Wb), (pB[dh],))
            m1, m2 = m1s[dh], m2s[dh]
            K.tt(m1[:], sg[:, dh * 512:(dh + 1) * 512], pA[dh][:], ALU.mult, (sg, pA[dh]), (m1,))
            K.tt(m2[:], sg[:, 1024 + dh * 512:1024 + (dh + 1) * 512], pB[dh][:], ALU.mult, (sg, pB[dh]), (m2,))
            K.tt(mt[:, sl], m1[:], m2[:], ALU.add, (m1, m2), (mt,))

    def stage2(ti):
        t0 = ti * 128
        r = ti % 2
        xt = xts[ti % 3]
        K.load("sp", xt, xt[:], sc["x1"][t0:t0 + 128, :])
        q3 = ti % 3
        transpose_bf(K, mts[q3], mts[q3], 8, idb, pT[0], mTs[q3][:], mTs[q3], eng="act")
        for dh in range(2):
            for kc in range(8):
                K.mm(pO[dh][:], mTs[q3][:, kc, :], Wo[:, kc, dh * 512:(dh + 1) * 512], kc == 0, kc == 7, (mTs[q3], Wo), (pO[dh],))
        outt = outs[ti % 3]
        post_norm_residual(K, pO, xt, gbc, sts[r], junk, outt)
        K.store("pool", outt, sc["x2"][t0:t0 + 128, :], outt[:])

    prep(0)
    for ti in range(ntile):
        if ti >= 2:
            stage2(ti - 2)
        if ti + 1 < ntile:
            prep(ti + 1)
        stage1(ti)
    stage2(ntile - 2)
    stage2(ntile - 1)


def phase_ple(K, io, sc, xin, yout):
    K.reset()
    idb = load_ident(K, io)
    gpre = K.alloc([128, 8], F32)
    K.load("sp", gpre, gpre[:], io["ple_norm_pre"])
    gbc = K.alloc([128, D], F32)
    K.load("sp", gbc, gbc[:], io["ple_norm_post"].partition_broadcast(128))
    Wg = K.alloc([128, 8, D], BF16)
    Wp = K.alloc([128, 2, D], BF16)
    load_weight_bf16(K, Wg, io["w_ple_gate"], 8, gpre)
    load_weight_bf16(K, Wp, io["w_ple_proj"], 2)
    xts = K.ring(4, [128, D], F32)
    pts = K.ring(4, [128, PLE], F32)
    pbs = K.ring(2, [128, PLE], BF16)
    xns = K.ring(2, [128, D], BF16)
    xnTs = K.ring(2, [128, 8, 128], BF16)
    pTs = K.ring(2, [128, 2, 128], BF16)
    gs = K.ring(2, [128, D], F32)
    prods = K.ring(2, [128, D], F32)
    outs = K.ring(3, [128, D], F32)
    sts = K.ring(2, [128, 4], F32)
    junk = K.alloc([128, D], BF16)
    pT = [K.bank(0), K.bank(7)]
    pG = [K.bank(1), K.bank(2)]
    pP = [K.bank(3), K.bank(4)]
    st2 = K.ring(2, [128, 4], F32)
    junk2 = K.alloc([128, D], BF16)
    sgtmp = K.ring(2, [128, 512], F32)
    ntile = NT // 128

    def prep_a(ti):
        t0 = ti * 128
        r = ti % 2
        xt, pt, pb = xts[ti % 4], pts[ti % 4], pbs[r]
        K.load("sp", xt, xt[:], xin[t0:t0 + 128, :])
        K.load("sp", pt, pt[:], io["p"][t0:t0 + 128, :])
        norm_a(K, xt, sts[r], xns[r], junk, eng="dve")
        K.cp(pb[:], pt[:], (pt,), (pb,))

    def prep_b(ti):
        r = ti % 2
        norm_b(K, xns[r], idb, xnTs[r][:], xnTs[r], pT[0])
        transpose_bf(K, pbs[r], pbs[r], 2, idb, pT[1], pTs[r][:], pTs[r])

    prep_a(0)
    prep_b(0)
    for ti in range(ntile):
        t0 = ti * 128
        r = ti % 2
        xt = xts[ti % 4]
        g, prod = gs[r], prods[r]
        if ti + 1 < ntile:
            prep_a(ti + 1)
        for dh in range(2):
            sl = slice(dh * 512, (dh + 1) * 512)
            pg_, pp_ = pG[dh], pP[dh]
            for kc in range(8):
                K.mm(pg_[:], xnTs[r][:, kc, :], Wg[:, kc, sl], kc == 0, kc == 7, (xnTs[r], Wg), (pg_,))
            for kc in range(2):
                K.mm(pp_[:], pTs[r][:, kc, :], Wp[:, kc, sl], kc == 0, kc == 1, (pTs[r], Wp), (pp_,))
            K.sigmoid(g[:, sl], pg_[:], (pg_,), (g,), sgtmp[dh])
            K.tt(prod[:, sl], g[:, sl], pp_[:], ALU.mult, (g, pp_), (prod,))
        if ti + 1 < ntile:
            prep_b(ti + 1)
        st = st2[r]
        K.act(junk2[:], prod[:], AF.Square, (prod,), (junk2, st), accum_out=st[:, 0:1])
        K.rstd_from_ssq(st, D)
        outt = outs[ti % 3]
        K.stt(outt[:], prod[:], st[:, 1:2], gbc[:], ALU.mult, ALU.mult, (prod, st, gbc), (outt,))
        K.tt(outt[:], outt[:], xt[:], ALU.add, (outt, xt), (outt,))
        K.store("pool", outt, yout[t0:t0 + 128, :], outt[:])


TWO_PI = 2.0 * math.pi


def sin_reduced(K, tmp, ki, kf, dst, dst_tile):
    K.ts(ki[:], tmp[:], 1.0 / TWO_PI, None, ALU.mult, None, (tmp,), (ki,))
    K.cp(kf[:], ki[:], (ki,), (kf,))
    K.stt(tmp[:], kf[:], -TWO_PI, tmp[:], ALU.mult, ALU.add, (kf, tmp), (tmp,))
    K.ts(tmp[:], tmp[:], -3.1415925, 3.1415925, ALU.max, ALU.min, (tmp,), (tmp,))
    K.act(dst, tmp[:], AF.Sin, (tmp,), (dst_tile,))


def phase_filt1(K, io, sc):
    K.reset()
    zT = K.alloc([33, L], F32)
    K.load("sp", zT, zT[:], io["filt_zT"])
    w1 = K.alloc([33, 64], F32)
    w2 = K.alloc([64, 64], F32)
    w3 = K.alloc([64, 2048], F32)
    K.load("sp", w1, w1[:], io["filt_w1"])
    K.load("sp", w2, w2[:], io["filt_w2"])
    K.load("sp", w3, w3[:], io["filt_w3"])
    cols = K.alloc([64, 4], F32)
    for i, nm in enumerate(("filt_b1", "filt_freq1", "filt_b2", "filt_freq2")):
        K.load("sp", cols, cols[:, i:i + 1], io[nm])
    negt = K.alloc([128, 32], F32)
    K.load("sp", negt, negt[:], io["negt"])
    absd = K.alloc([128, 2048], F32)
    K.load("sp", absd, absd[:], io["filt_deltas"].partition_broadcast(128))
    K.stt(absd[:], absd[:], -1.0, absd[:], ALU.mult, ALU.max, (absd,), (absd,))
    ones = K.alloc([128, 128], F32)
    K.memset(ones[:], 1.0, (ones,))
    h1T = K.alloc([64, L], F32)
    h2T = K.alloc([64, L], F32)
    tmp = K.alloc([64, 512], F32)
    kf = K.alloc([64, 512], F32)
    ki = Tile(K.alloc([64, 512], F32).ap.bitcast(mybir.dt.int32), None)
    b1c, f1c, b2c, f2c = (cols[:, i:i + 1] for i in range(4))
    for c in range(8):
        ps = K.bank(c % 2)
        K.mm(ps[0:64, :], w1[:, :], zT[:, c * 512:(c + 1) * 512], True, True, (w1, zT), (ps,))
        K.ts(tmp[:], ps[0:64, :], cols[:, 0:1], cols[:, 1:2], ALU.add, ALU.mult, (ps, cols), (tmp,))
        sin_reduced(K, tmp, ki, kf, h1T[:, c * 512:(c + 1) * 512], h1T)
    for c in range(8):
        ps = K.bank(c % 2)
        K.mm(ps[0:64, :], w2[:, :], h1T[:, c * 512:(c + 1) * 512], True, True, (w2, h1T), (ps,))
        K.ts(tmp[:], ps[0:64, :], cols[:, 2:3], cols[:, 3:4], ALU.add, ALU.mult, (ps, cols), (tmp,))
        sin_reduced(K, tmp, ki, kf, h2T[:, c * 512:(c + 1) * 512], h2T)
    decs = K.ring(2, [128, 2048], F32)
    hfs = K.ring(2, [128, 2048], F32)
    As = K.ring(2, [128, 1024], F32)
    a1s = K.ring(2, [128, 512], F32)
    Ets = K.ring(2, [128, 1024], BF16)
    Ots = K.ring(2, [128, 1024], BF16)
    pA = [K.bank(6), K.bank(7)]
    for j in range(32):
        r = j % 2
        dec, hf, A, Et, Ot, a1 = decs[r], hfs[r], As[r], Ets[r], Ots[r], a1s[r]
        K.act(dec[:], absd[:], AF.Exp, (absd, negt), (dec,), scale=negt[:, j:j + 1])
        for cc in range(4):
            ps = K.bank(2 + cc)
            K.mm(ps[:], h2T[:, j * 128:(j + 1) * 128], w3[:, cc * 512:(cc + 1) * 512], True, True, (h2T, w3), (ps,))
            K.tt(hf[:, cc * 512:(cc + 1) * 512], ps[:], dec[:, cc * 512:(cc + 1) * 512], ALU.mult, (ps, dec), (hf,))
        for o in range(2):
            f_ = hf[:, o * 1024:o * 1024 + 512]
            b_ = hf[:, o * 1024 + 512:o * 1024 + 1024]
            osl = slice(o * 512, (o + 1) * 512)
            K.tt(Et[:, osl], f_, b_, ALU.add, (hf,), (Et,))
            K.tt(Ot[:, osl], f_, b_, ALU.subtract, (hf,), (Ot,))
            K.act(a1[:], f_, AF.Abs, (hf,), (a1,))
            K.act(A[:, osl], b_, AF.Abs, (hf,), (A,))
            K.tt(A[:, osl], A[:, osl], a1[:], ALU.add, (A, a1), (A,))
            if j == 0:
                K.tt(A[0:1, osl], f_[0:1, :], b_[0:1, :], ALU.add, (hf, A), (A,))
                K.stt(A[0:1, osl], A[0:1, osl], -1.0, A[0:1, osl], ALU.mult, ALU.max, (A,), (A,))
        for o in range(2):
            K.mm(pA[o][:], ones[:], A[:, o * 512:(o + 1) * 512], j == 0, j == 31, (ones, A), (pA[o],))
        K.store("pool", Et, sc["Ebf"][j * 128:(j + 1) * 128, :], Et[:])
        K.store("pool", Ot, sc["Obf"][j * 128:(j + 1) * 128, :], Ot[:])
    rn = K.alloc([128, 1024], F32)
    for o in range(2):
        K.recip(rn[:, o * 512:(o + 1) * 512], pA[o][:], (pA[o],), (rn,))
    K.store("pool", rn, sc["rn"], rn[:])


def fold_inplace(K, X, ncol, Rb, Eb_, banks):
    bi = 0
    for j in range(16):
        for h in range(ncol // 512):
            cs = slice(h * 512, (h + 1) * 512)
            pb = banks[bi % len(banks)]
            bi += 1
            K.mm(pb[:], Rb[:], X[:, 31 - j, cs], True, j == 0, (Rb, X), (pb,))
            if j > 0:
                K.mm(pb[:], Eb_[:], X[:, 32 - j, cs], False, True, (Eb_, X), (pb,))
            K.tt(X[:, 32 - j, cs], X[:, j, cs], pb[:], ALU.subtract, (X, pb), (X,))
            K.tt(X[:, j, cs], X[:, j, cs], pb[:], ALU.add, (X, pb), (X,))
    for h in range(ncol // 512):
        cs = slice(h * 512, (h + 1) * 512)
        pb = banks[bi % len(banks)]
        bi += 1
        K.mm(pb[:], Eb_[:], X[:, 16, cs], True, True, (Eb_, X), (pb,))
        K.cp(X[:, 16, cs], pb[:], (pb,), (X,))


def fwd_lists(par):
    e_list = [(j, j) for j in range(17)]
    d_list = [(j, 32 - j) for j in range(16)]
    return (e_list, d_list) if par == 0 else (d_list, e_list)


def load_tab(K, tile, tab, idx, nj):
    K.load("sp", tile, tile[:, 0:nj, :], tab[idx].rearrange("p (j q) -> p j q", j=nj))


def phase_filt2_sconv(K, io, sc):
    K.reset()
    XE = K.alloc([128, 33, 512], BF16)
    XO = K.alloc([128, 33, 512], BF16)
    rn = K.alloc([128, 1024], F32)
    K.load("sp", rn, rn[:], sc["rn"])
    wk = K.alloc([128, 32], F32)
    K.load("sp", wk, wk[:], io["wk"])
    alt = K.alloc([128, 128], BF16)
    K.load("sp", alt, alt[:], io["alt"])
    Rb = K.alloc([128, 128], BF16)
    Eb_ = K.alloc([128, 128], BF16)
    K.load("sp", Rb, Rb[:], io["rev_rb"])
    K.load("sp", Eb_, Eb_[:], io["rev_eb"])
    hns = K.ring(2, [128, 512], F32)
    Cs = K.ring(3, [128, 17, 128], BF16)
    Ss = K.ring(3, [128, 17, 128], BF16)
    Hcs = K.ring(2, [128, 512], F32)
    Hss = K.ring(2, [128, 512], F32)
    wbc = K.alloc([128, 3, 1536], F32)
    bbc = K.alloc([128, 1536], F32)
    K.load("sp", wbc, wbc[:], io["hy_short_w"].partition_broadcast(128).rearrange("p o (k c) -> p (o k) c", k=3))
    K.load("sp", bbc, bbc[:], io["hy_short_b"].partition_broadcast(128))
    a0s = K.ring(2, [128, 1536], F32)
    a1s = K.ring(2, [128, 1536], F32)
    a2s = K.ring(2, [128, 1536], F32)
    hvbs = K.ring(2, [128, 512], BF16)

    def sconv_tile(ti):
        b, j = divmod(ti, 32)
        t0 = ti * 128
        r = ti % 2
        a0, a1, a2, hvb = a0s[r], a1s[r], a2s[r], hvbs[r]
        if j == 0:
            K.memset(a0[:], 0.0, (a0,), eng="pool")
            K.load("sp", a0, a0[1:128, :], sc["hy"][t0:t0 + 127, :])
        else:
            K.load("sp", a0, a0[:], sc["hy"][t0 - 1:t0 + 127, :])
        K.load("sp", a1, a1[:], sc["hy"][t0:t0 + 128, :])
        if j == 31:
            K.memset(a2[:], 0.0, (a2,), eng="pool")
            K.load("sp", a2, a2[0:127, :], sc["hy"][t0 + 1:t0 + 128, :])
        else:
            K.load("sp", a2, a2[:], sc["hy"][t0 + 1:t0 + 129, :])
        K.tt(a0[:], a0[:], wbc[:, 0, :], ALU.mult, (a0, wbc), (a0,), eng="pool")
        K.tt(a2[:], a2[:], wbc[:, 2, :], ALU.mult, (a2, wbc), (a2,), eng="pool")
        K.tt(a1[:], a1[:], wbc[:, 1, :], ALU.mult, (a1, wbc), (a1,))
        K.tt(a1[:], a1[:], bbc[:], ALU.add, (a1, bbc), (a1,))
        K.tt(a1[:], a1[:], a0[:], ALU.add, (a1, a0), (a1,))
        K.tt(a1[:], a1[:], a2[:], ALU.add, (a1, a2), (a1,))
        K.act(hvb[:], a1[:, 0:512], AF.Copy, (a1,), (hvb,))
        K.store("pool", a1, sc["z"][t0:t0 + 128, :], a1[:])
        K.store("pool", hvb, sc["hvb"][t0:t0 + 128, :], hvb[:])

    slc = 0
    for o in range(2):
        osl = slice(o * 512, (o + 1) * 512)
        K.load("sp", XE, XE[:, 0:32, :], sc["Ebf"][:, osl].rearrange("(j p) c -> p j c", p=128))
        K.load("sp", XO, XO[:, 0:32, :], sc["Obf"][:, osl].rearrange("(j p) c -> p j c", p=128))
        ps = K.bank(0)
        hn = hns[o]
        for j in range(32):
            K.mm(ps[:], alt[:], XE[:, j, :], j == 0, j == 31, (alt, XE), (ps,))
        K.stt(hn[:], ps[:], 1.0 / (2 * L), rn[:, osl], ALU.mult, ALU.mult, (ps, rn), (hn,))
        K.store("pool", hn, sc["HN"][:, osl], hn[:])
        fold_inplace(K, XE, 512, Rb, Eb_, [K.bank(0), K.bank(1)])
        fold_inplace(K, XO, 512, Rb, Eb_, [K.bank(0), K.bank(1)])
        for c in range(32):
            r = c % 2
            C_, S_ = Cs[slc % 3], Ss[slc % 3]
            slc += 1
            load_tab(K, C_, io["tfc"], c, 17)
            load_tab(K, S_, io["tfs"], c, 17)
            clist, slist = fwd_lists(c // 16)
            pc = K.bank(2 + r)
            for i, (j, slot) in enumerate(clist):
                K.mm(pc[:], C_[:, j, :], XE[:, slot, :], i == 0, i == len(clist) - 1, (C_, XE), (pc,))
            K.stt(Hcs[r][:], pc[:], wk[:, c:c + 1], rn[:, osl], ALU.mult, ALU.mult, (pc, wk, rn), (Hcs[r],))
            K.store("pool", Hcs[r], sc["Hc"][c][:, osl], Hcs[r][:])
            pc = K.bank(4 + r)
            for i, (j, slot) in enumerate(slist):
                K.mm(pc[:], S_[:, j, :], XO[:, slot, :], i == 0, i == len(slist) - 1, (S_, XO), (pc,))
            K.stt(Hss[r][:], pc[:], wk[:, c:c + 1], rn[:, osl], ALU.mult, ALU.mult, (pc, wk, rn), (Hss[r],))
            K.store("pool", Hss[r], sc["Hs"][c][:, osl], Hss[r][:])
            sconv_tile(o * 32 + c)


def phase_hyena(K, io, sc):
    K.reset()
    alt = K.alloc([128, 128], BF16)
    K.load("sp", alt, alt[:], io["alt"])
    Rb = K.alloc([128, 128], BF16)
    Eb_ = K.alloc([128, 128], BF16)
    R32 = K.alloc([128, 128], F32)
    E32 = K.alloc([128, 128], F32)
    K.load("sp", Rb, Rb[:], io["rev_rb"])
    K.load("sp", Eb_, Eb_[:], io["rev_eb"])
    K.load("sp", R32, R32[:], io["rev_r"])
    K.load("sp", E32, E32[:], io["rev_e"])
    sgn = K.alloc([128, 1], F32)
    K.load("sp", sgn, sgn[:], io["sgn"])
    X = K.alloc([128, 33, 512], BF16)
    Y = K.alloc([128, 32, 2, 512], BF16)
    Cs = K.ring(3, [128, 32, 128], BF16)
    Ss = K.ring(3, [128, 32, 128], BF16)
    Hts = K.ring(2, [128, 2, 256], F32)
    hn = K.alloc([128, 256], F32)
    YN = K.alloc([128, 512], F32)
    bias = K.alloc([128, 256], F32)
    t1s = K.ring(2, [128, 512], F32)
    t2s = K.ring(2, [128, 512], F32)
    x32s = K.ring(3, [128, 512], F32)
    gts = K.ring(3, [128, 512], F32)
    cvs = K.ring(3, [128, 512], F32)
    obs = K.ring(3, [128, 512], BF16)
    Ssbs = K.ring(2, [128, 512], F32)
    his = K.ring(2, [128, 512], F32)
    pX = [[K.bank(0), K.bank(1)], [K.bank(2), K.bank(3)]]
    pSA = [[K.bank(0), K.bank(1)], [K.bank(2), K.bank(3)]]
    pR = [K.bank(4), K.bank(5)]
    pN = K.bank(6)
    slc = 0
    ec = 0
    v3 = lambda ap: ap.rearrange("p (b c) -> p b c", b=2)
    sgr = []
    sgi = [0]
    NSG = 2 * (NT // 128)

    def gate_sigmoid_tile():
        hi_ = sgi[0]
        if hi_ >= NSG:
            return
        sgi[0] += 1
        ti, hc_ = divmod(hi_, 2)
        t = sgr[hi_ % 2]
        dst = sc["SG"][ti * 128:(ti + 1) * 128, hc_ * 1024:(hc_ + 1) * 1024]
        K.load("sp", t, t[:], dst)
        K.sigmoid(t[:], t[:], (t,), (t,), t)
        K.store("pool", t, dst, t[:])

    for o in range(2):
        src_bf = sc["hvb"] if o == 0 else sc["z1b"]
        src32 = sc["z"] if o == 0 else sc["z1"]
        for half in range(2):
            csl = slice(half * 256, (half + 1) * 256)
            hsl = slice(o * 512 + half * 256, o * 512 + (half + 1) * 256)
            gsl = slice(512 + o * 512 + half * 256, 512 + o * 512 + (half + 1) * 256)
            xv = X[:, 0:32, :].rearrange("p j (b c) -> p j b c", b=2)
            for b in range(NB):
                K.load("sp", X, xv[:, :, b, :], src_bf[b * L:(b + 1) * L, csl].rearrange("(j p) c -> p j c", p=128))
            K.load("sp", hn, hn[:], sc["HN"][:, hsl])
            K.load("sp", bias, bias[:], io["hy_bias"][:, hsl].partition_broadcast(128))
            for j in range(32):
                K.mm(pN[:], alt[:], X[:, j, :], j == 0, j == 31, (alt, X), (pN,))
            K.tt(v3(YN[:]), v3(pN[:]), hn[:].unsqueeze(1).to_broadcast([128, 2, 256]), ALU.mult, (pN, hn), (YN,))
            fold_inplace(K, X, 512, Rb, Eb_, [K.bank(4), K.bank(5)])
            tabs_f = {}

            def pre_f(c):
                nonlocal slc
                C_, S_ = Cs[slc % 3], Ss[slc % 3]
                slc += 1
                load_tab(K, C_, io["tfc"], c, 17)
                load_tab(K, S_, io["tfs"], c, 17)
                tabs_f[c] = (C_, S_)

            pre_f(0)
            for c in range(32):
                r = c % 2
                if c + 1 < 32:
                    pre_f(c + 1)
                C_, S_ = tabs_f.pop(c)
                Ht = Hts[r]
                K.load("sp", Ht, Ht[:, 0, :], sc["Hc"][c][:, hsl])
                K.load("sp", Ht, Ht[:, 1, :], sc["Hs"][c][:, hsl])
                pc, ps_ = pX[r]
                clist, slist = fwd_lists(c // 16)
                for i, (j, slot) in enumerate(clist):
                    K.mm(pc[:], C_[:, j, :], X[:, slot, :], i == 0, i == len(clist) - 1, (C_, X), (pc,))
                for i, (j, slot) in enumerate(slist):
                    K.mm(ps_[:], S_[:, j, :], X[:, slot, :], i == 0, i == len(slist) - 1, (S_, X), (ps_,))
                hc = Ht[:, 0, :].unsqueeze(1).to_broadcast([128, 2, 256])
                hs = Ht[:, 1, :].unsqueeze(1).to_broadcast([128, 2, 256])
                t1, t2 = t1s[r], t2s[r]
                K.tt(v3(t1[:]), v3(pc[:]), hc, ALU.mult, (pc, Ht), (t1,))
                K.tt(v3(t2[:]), v3(ps_[:]), hs, ALU.mult, (ps_, Ht), (t2,))
                K.tt(Y[:, c, 0, :], t1[:], t2[:], ALU.subtract, (t1, t2), (Y,))
                K.tt(v3(t1[:]), v3(pc[:]), hs, ALU.mult, (pc, Ht), (t1,))
                K.tt(v3(t2[:]), v3(ps_[:]), hc, ALU.mult, (ps_, Ht), (t2,))
                K.tt(Y[:, c, 1, :], t1[:], t2[:], ALU.add, (t1, t2), (Y,))

            def epilogue(ch, src_ap, src_tile):
                nonlocal ec
                x32, gt, cv, ob = x32s[ec % 3], gts[ec % 3], cvs[ec % 3], obs[ec % 3]
                ec += 1
                for b in range(NB):
                    t0 = b * L + ch * 128
                    K.load("sp", x32, x32[:, b * 256:(b + 1) * 256], src32[t0:t0 + 128, csl])
                    K.load("sp", gt, gt[:, b * 256:(b + 1) * 256], sc["z"][t0:t0 + 128, gsl])
                K.stt(cv[:], YN[:], sgn[:, 0:1], src_ap, ALU.mult, ALU.add, (YN, sgn, src_tile), (cv,))
                K.tt(v3(x32[:]), v3(x32[:]), bias[:].unsqueeze(1).to_broadcast([128, 2, 256]), ALU.mult, (x32, bias), (x32,), eng="pool")
                K.tt(cv[:], cv[:], x32[:], ALU.add, (cv, x32), (cv,))
                if o == 0:
                    K.tt(cv[:], cv[:], gt[:], ALU.mult, (cv, gt), (cv,))
                    K.act(ob[:], cv[:], AF.Copy, (cv,), (ob,))
                    for b in range(NB):
                        t0 = b * L + ch * 128
                        K.store("pool", cv, sc["z1"][t0:t0 + 128, csl], cv[:, b * 256:(b + 1) * 256])
                        K.store("pool", ob, sc["z1b"][t0:t0 + 128, csl], ob[:, b * 256:(b + 1) * 256])
                else:
                    K.tt(ob[:], cv[:], gt[:], ALU.mult, (cv, gt), (ob,))
                    for b in range(NB):
                        t0 = b * L + ch * 128
                        K.store("pool", ob, sc["ya"][t0:t0 + 128, csl], ob[:, b * 256:(b + 1) * 256])

            tabs_i = {}

            def pre_i(tc):
                nonlocal slc
                C_, S_ = Cs[slc % 3], Ss[slc % 3]
                slc += 1
                load_tab(K, C_, io["tic"], tc, 32)
                load_tab(K, S_, io["tis"], tc, 32)
                tabs_i[tc] = (C_, S_)

            pre_i(16)
            for it, tc in enumerate(range(16, -1, -1)):
                r = it % 2
                if tc - 1 >= 0:
                    pre_i(tc - 1)
                C_, S_ = tabs_i.pop(tc)
                pS_, pA_ = pSA[r]
                for c in range(32):
                    tab, cs_ = (C_, 0) if c < 16 else (S_, 1)
                    K.mm(pS_[:], tab[:, c, :], Y[:, c, cs_, :], c == 0, c == 31, (tab, Y), (pS_,))
                Ssb, hi = Ssbs[r], his[r]
                if tc == 16:
                    K.act(hi[:], pS_[:], AF.Copy, (pS_,), (hi,))
                    continue
                for i, c in enumerate(list(range(16, 32)) + list(range(0, 16))):
                    tab, cs_ = (C_, 0) if c >= 16 else (S_, 1)
                    K.mm(pA_[:], tab[:, c, :], Y[:, c, cs_, :], i == 0, i == 31, (tab, Y), (pA_,))
                K.act(Ssb[:], pS_[:], AF.Copy, (pS_,), (Ssb,))
                K.tt(hi[:], Ssb[:], pA_[:], ALU.subtract, (Ssb, pA_), (hi,))
                K.tt(Ssb[:], Ssb[:], pA_[:], ALU.add, (Ssb, pA_), (Ssb,))
                hi_prev = his[(it - 1) % 2]
                pr = pR[r]
                K.mm(pr[:], R32[:], hi[:], True, False, (R32, hi), (pr,))
                K.mm(pr[:], E32[:], hi_prev[:], False, True, (E32, hi_prev), (pr,))
                epilogue(tc, Ssb[:], Ssb)
                epilogue(31 - tc, pr[:], pr)
        K.S.barrier()


class IO:
    SHAPES = {
        "x": ([NT, D], F32), "p": ([NT, PLE], F32), "ident": ([128, 128], F32),
    }
    SHAPES.update({
        "mix_norm_pre": ([128, 8], F32), "w_in": ([D, INC], F32), "q_norm": ([1, 64], F32), "k_norm": ([1, 64], F32),
        "rope_cos": ([128, 1024], F32), "rope_sin": ([128, 1024], F32),
    })
    SHAPES.update({
        "mix_norm_post": ([1, D], F32), "w_hy_out": ([512, D], F32), "w_att_out": ([512, D], F32), "w_out": ([D, D], F32),
        "ple_norm_pre": ([128, 8], F32), "ple_norm_post": ([1, D], F32), "w_ple_gate": ([D, D], F32), "w_ple_proj": ([PLE, D], F32),
    })
    SHAPES.update({
        "filt_zT": ([33, L], F32), "filt_w1": ([33, 64], F32), "filt_w2": ([64, 64], F32), "filt_w3": ([64, 2048], F32),
        "filt_b1": ([64, 1], F32), "filt_freq1": ([64, 1], F32), "filt_b2": ([64, 1], F32), "filt_freq2": ([64, 1], F32),
        "filt_deltas": ([1, 2048], F32), "negt": ([128, 32], F32), "wk": ([128, 32], F32), "alt": ([128, 128], BF16),
        "sgn": ([128, 1], F32), "tfc": ([32, 128, 17 * 128], BF16), "tfs": ([32, 128, 17 * 128], BF16),
        "tic": ([17, 128, 4096], BF16), "tis": ([17, 128, 4096], BF16),
        "rev_r": ([128, 128], F32), "rev_e": ([128, 128], F32), "rev_rb": ([128, 128], BF16), "rev_eb": ([128, 128], BF16),
        "hy_bias": ([1, 1024], F32), "hy_short_w": ([1, 4608], F32), "hy_short_b": ([1, 1536], F32),
    })
    for _p in ("ffn1", "ffn2"):
        SHAPES[_p + "_norm_pre"] = ([128, 8], F32)
        SHAPES[_p + "_norm_post"] = ([1, D], F32)
        SHAPES[_p + "_w_gate"] = ([D, FF], F32)
        SHAPES[_p + "_w_up"] = ([D, FF], F32)
        SHAPES[_p + "_w_down"] = ([FF, D], F32)

    def __init__(self, nc):
        self.nc = nc
        self.d = {}

    def __getitem__(self, name):
        if name not in self.d:
            shape, dt = self.SHAPES[name]
            self.d[name] = self.nc.dram_tensor(name, list(shape), dt, kind="ExternalInput").ap()
        return self.d[name]


ALL_PHASES = ("ffn1", "win", "filt1", "filt2", "hyena", "attn", "merge", "ffn2", "ple")


def build_program(phases=ALL_PHASES, debug=()):
    nc = bass.Bass("TRN2", target_bir_lowering=False)
    io = IO(nc)
    y = nc.dram_tensor("y", [NT, D], F32, kind="ExternalOutput").ap()

    def scratch(name, shape, dt=F32):
        kind = "ExternalOutput" if name in debug else "Internal"
        return nc.dram_tensor(name, list(shape), dt, kind=kind).ap()

    sc = {}
    sc["x1"] = scratch("x1", [NT, D])
    sc["hy"] = scratch("hy", [NT, 1536])
    sc["SG"] = scratch("SG", [NT, 2048])
    sc["QT"] = scratch("QT", [NB, 8, 64, L], BF16)
    sc["KT"] = scratch("KT", [NB, 2, 64, L], BF16)
    sc["VA"] = scratch("VA", [NB, 2, L, 68], BF16)
    sc["ya"] = scratch("ya", [NT, 512], BF16)
    sc["yb"] = scratch("yb", [NT, 512], BF16)
    sc["x2"] = scratch("x2", [NT, D])
    sc["x3"] = scratch("x3", [NT, D])
    sc["Ebf"] = scratch("Ebf", [L, 1024], BF16)
    sc["Obf"] = scratch("Obf", [L, 1024], BF16)
    sc["rn"] = scratch("rn", [128, 1024])
    sc["HN"] = scratch("HN", [128, 1024])
    sc["Hc"] = scratch("Hc", [32, 128, 1024])
    sc["Hs"] = scratch("Hs", [32, 128, 1024])
    sc["z"] = scratch("z", [NT, 1536])
    sc["hvb"] = scratch("hvb", [NT, 512], BF16)
    sc["z1"] = scratch("z1", [NT, 512])
    sc["z1b"] = scratch("z1b", [NT, 512], BF16)
    x1 = sc["x1"]
    with contextlib.ExitStack() as st:
        arena = st.enter_context(nc.sbuf_tensor("arena", [128, ARENA_WORDS], F32))
        psum = st.enter_context(nc.psum_tensor("psum", [128, 4096], F32))
        banks = [psum[:, i * 512:(i + 1) * 512] for i in range(8)]
        K = KB(nc, arena, banks)
        K.psum = psum
        if "ffn1" in phases:
            phase_ffn(K, io, io["x"], y if phases[-1] == "ffn1" else x1, "ffn1")
        if "win" in phases:
            phase_win(K, io, sc)
        if "filt1" in phases:
            phase_filt1(K, io, sc)
        if "filt2" in phases:
            phase_filt2_sconv(K, io, sc)
        if "hyena" in phases:
            phase_hyena(K, io, sc)
        if "attn" in phases:
            phase_attn(K, io, sc)
        if "merge" in phases:
            phase_merge(K, io, sc)
        if "ffn2" in phases:
            phase_ffn(K, io, sc["x2"], sc["x3"], "ffn2", preloaded=("merge" in phases))
        if "ple" in phases:
            phase_ple(K, io, sc, sc["x3"], y)
        K.S.barrier()
        K.S.emit()
    return nc, sorted(io.d.keys())


_CONSTS = {}


def _hyena_consts():
    if _CONSTS:
        return _CONSTS
    f32 = np.float32
    N = 2 * L
    bf = ml_dtypes.bfloat16
    cidx = np.arange(32)
    kk = (2 * (128 * (cidx[:, None] % 16) + np.arange(128)[None, :]) + (cidx[:, None] // 16)).astype(np.int64)
    nn = (128 * np.arange(17)[:, None] + np.arange(128)[None, :]).astype(np.int64)
    m = (nn[None, :, :, None] * kk[:, None, None, :]) % N
    ang = (2.0 * np.pi / N) * m.astype(np.float64)
    _CONSTS["tfc"] = np.ascontiguousarray(np.cos(ang).astype(f32).transpose(0, 2, 1, 3).reshape(32, 128, 17 * 128).astype(bf))
    _CONSTS["tfs"] = np.ascontiguousarray(np.sin(ang).astype(f32).transpose(0, 2, 1, 3).reshape(32, 128, 17 * 128).astype(bf))
    tt_ = (128 * np.arange(17)[:, None] + np.arange(128)[None, :]).astype(np.int64)
    m = (kk[None, :, :, None] * tt_[:, None, None, :]) % N
    ang = (2.0 * np.pi / N) * m.astype(np.float64)
    _CONSTS["tic"] = np.ascontiguousarray(np.cos(ang).astype(f32).transpose(0, 2, 1, 3).reshape(17, 128, 32 * 128).astype(bf))
    _CONSTS["tis"] = np.ascontiguousarray(np.sin(ang).astype(f32).transpose(0, 2, 1, 3).reshape(17, 128, 32 * 128).astype(bf))
    del m, ang
    wk2 = np.where(kk == 0, 1.0 / N, 2.0 / N).astype(f32)
    _CONSTS["wk"] = np.ascontiguousarray(wk2.T)
    Rm = np.zeros((128, 128), f32)
    for q in range(1, 128):
        Rm[q, 128 - q] = 1.0
    Em = np.zeros((128, 128), f32)
    Em[0, 0] = 1.0
    _CONSTS["rev_r"] = Rm
    _CONSTS["rev_e"] = Em
    _CONSTS["rev_rb"] = np.ascontiguousarray(Rm.astype(bf))
    _CONSTS["rev_eb"] = np.ascontiguousarray(Em.astype(bf))
    t = np.linspace(0.0, 1.0, L, dtype=f32)
    w = ((2.0 * math.pi / L) * np.arange(L, dtype=f32)).astype(f32)
    bands = np.linspace(1e-4, 15.0, 16, dtype=f32)
    a2 = (w[:, None] * bands[None, :]).astype(f32)
    z = np.concatenate([t[:, None], np.cos(a2), -np.sin(a2)], axis=-1).astype(f32)
    _CONSTS["filt_zT"] = np.ascontiguousarray(z.T)
    _CONSTS["negt"] = np.ascontiguousarray((-t).reshape(32, 128).T)
    sg = np.where(np.arange(128) % 2 == 0, 1.0, -1.0).astype(f32)
    _CONSTS["alt"] = np.ascontiguousarray(np.repeat(sg[:, None], 128, axis=1).astype(ml_dtypes.bfloat16))
    _CONSTS["sgn"] = np.ascontiguousarray(sg.reshape(128, 1))
    return _CONSTS


def _col128(v):
    v = np.asarray(v, np.float32).reshape(-1, 128)
    return np.ascontiguousarray(v.T)


def make_in_maps(inputs):
    f32 = np.float32
    shared = {"ident": np.eye(128, dtype=f32)}
    for pfx in ("ffn1", "ffn2"):
        shared[pfx + "_norm_pre"] = _col128(inputs[pfx + "_norm_pre"][0])
        shared[pfx + "_norm_post"] = np.ascontiguousarray(inputs[pfx + "_norm_post"][0].reshape(1, D), f32)
        shared[pfx + "_w_gate"] = np.ascontiguousarray(inputs[pfx + "_w_gate"][0], f32)
        shared[pfx + "_w_up"] = np.ascontiguousarray(inputs[pfx + "_w_up"][0], f32)
        shared[pfx + "_w_down"] = np.ascontiguousarray(inputs[pfx + "_w_down"][0], f32)
    shared["mix_norm_post"] = np.ascontiguousarray(inputs["mix_norm_post"][0].reshape(1, D), f32)
    shared["ple_norm_post"] = np.ascontiguousarray(inputs["ple_norm_post"][0].reshape(1, D), f32)
    shared["ple_norm_pre"] = _col128(inputs["ple_norm_pre"][0])
    for nm in ("w_hy_out", "w_att_out", "w_out", "w_ple_gate", "w_ple_proj"):
        shared[nm] = np.ascontiguousarray(inputs[nm][0], f32)
    shared["mix_norm_pre"] = _col128(inputs["mix_norm_pre"][0])
    shared["w_in"] = np.ascontiguousarray(inputs["w_in"][0], f32)
    shared["q_norm"] = np.ascontiguousarray(inputs["q_norm"][0].reshape(1, 64), f32)
    shared["k_norm"] = np.ascontiguousarray(inputs["k_norm"][0].reshape(1, 64), f32)
    tpos = np.arange(L)
    inv = (10000.0 ** (-np.arange(0, 32, 2, dtype=np.float32) / 32)).astype(f32)
    ang = np.concatenate([(tpos // 64).astype(f32)[:, None] * inv, (tpos % 64).astype(f32)[:, None] * inv], axis=-1).astype(f32)
    shared["rope_cos"] = np.ascontiguousarray(np.cos(ang).astype(f32).reshape(32, 128, 32).transpose(1, 0, 2).reshape(128, 1024))
    shared["rope_sin"] = np.ascontiguousarray(np.sin(ang).astype(f32).reshape(32, 128, 32).transpose(1, 0, 2).reshape(128, 1024))
    shared.update(_hyena_consts())
    for nm in ("filt_w1", "filt_w2", "filt_w3"):
        shared[nm] = np.ascontiguousarray(inputs[nm][0], f32)
    for nm in ("filt_b1", "filt_freq1", "filt_b2", "filt_freq2"):
        shared[nm] = np.ascontiguousarray(inputs[nm][0].reshape(64, 1), f32)
    shared["filt_deltas"] = np.ascontiguousarray(inputs["filt_deltas"][0].reshape(1, 2048), f32)
    shared["hy_bias"] = np.ascontiguousarray(inputs["hy_bias"][0].reshape(1, 1024), f32)
    shared["hy_short_w"] = np.ascontiguousarray(inputs["hy_short_w"][0].reshape(1, 4608), f32)
    shared["hy_short_b"] = np.ascontiguousarray(inputs["hy_short_b"][0].reshape(1, 1536), f32)
    maps = []
    x = np.asarray(inputs["x"], f32)
    p = np.asarray(inputs["p"], f32)[0]
    for c in range(NCORES):
        m = dict(shared)
        m["x"] = np.ascontiguousarray(x[c * NB:(c + 1) * NB].reshape(NT, D))
        m["p"] = np.ascontiguousarray(p[c * NB:(c + 1) * NB].reshape(NT, PLE))
        maps.append(m)
    return maps


def kernel(**inputs):
    nc, used = build_program(phases=ALL_PHASES)
    maps = [{k: m[k] for k in used} for m in make_in_maps(inputs)]
    res = run_bass_kernel_spmd(nc, maps, core_ids=list(range(NCORES)))
    out = np.concatenate([np.asarray(r["y"], np.float32).reshape(NB, L, D) for r in res.results], axis=0)
    return out
```
